# Optimizing a Trainium2 kernel written in Bass

```python
import jax, jax.numpy as jnp
from jax import lax
import numpy as np

D_MODEL = 2048
BATCH = 2
SEQ = 4096
DEPTH = 1
DEC_BATCH = 32
DEC_SEQ = 16
PAST_LEN = 2048

CHUNK = 64
D_MIX = D_MODEL
D_A = D_MIX // 2
D_B = D_MIX - D_A
GMLP_CHUNK = 128
GMLP_GROUPS = 8
GMLP_GDIM = D_A // GMLP_GROUPS
LRU_HEADS = 8
LRU_HDIM = D_B // LRU_HEADS
LRU_CONV = 4
LRU_C = 8.0
N_MEM = 256
XA_HEADS = 4
XA_HDIM = D_MODEL // XA_HEADS
D_FF = 3 * D_MODEL
FFN_CONV = 3
D_IN = 2 * D_A + 2 * D_B
EPS = 1e-6

kernel_name = 'hymba_gmlp_rglru_streaming_step'


def rms_norm(x, g):
    xf = x.astype(jnp.float32)
    y = xf * lax.rsqrt(jnp.mean(xf * xf, axis=-1, keepdims=True) + EPS)
    return (y * g.astype(jnp.float32)).astype(x.dtype)


def causal_dwconv(x, state, w, b):
    width = w.shape[0]
    T = x.shape[1]
    xp = jnp.concatenate([state.astype(x.dtype), x], axis=1)
    y = xp[:, 0:T] * w[0]
    for k in range(1, width):
        y = y + xp[:, k:k + T] * w[k]
    return y + b, xp[:, T:]


def chunk_causal_mask(n):
    c = jnp.arange(n) // CHUNK
    return c[:, None] >= c[None, :]


def gmlp_spatial(v, w_s, b_s):
    n = v.shape[2]
    w = jnp.where(chunk_causal_mask(n)[None], w_s[:, :n, :n], 0.0).astype(v.dtype)
    out = jnp.einsum('gij,bnjgc->bnigc', w, v)
    return out + b_s[:, :n].T[None, None, :, :, None].astype(v.dtype)


def lin_combine(left, right):
    a_l, b_l = left
    a_r, b_r = right
    return a_l * a_r, a_r * b_l + b_r


def rglru(x, h0, w_a, b_a, w_x, b_x, lam):
    B, T, _ = x.shape
    xh = x.reshape(B, T, LRU_HEADS, LRU_HDIM)
    r = jax.nn.sigmoid((jnp.einsum('bthi,hij->bthj', xh, w_a).reshape(B, T, D_B) + b_a).astype(jnp.float32))
    i = jax.nn.sigmoid((jnp.einsum('bthi,hij->bthj', xh, w_x).reshape(B, T, D_B) + b_x).astype(jnp.float32))
    log_a = -LRU_C * r * jax.nn.softplus(-lam.astype(jnp.float32))
    a = jnp.exp(log_a)
    bterm = jnp.sqrt(-jnp.expm1(2.0 * log_a)) * (i * x.astype(jnp.float32))
    bterm = bterm.at[:, 0].add(a[:, 0] * h0.astype(jnp.float32))
    _, h = lax.associative_scan(lin_combine, (a, bterm), axis=1)
    return h.astype(x.dtype), h[:, -1].astype(h0.dtype)


def memory_kv(mem, g, w_kv):
    B, M, _ = mem.shape
    kv = rms_norm(mem, g) @ w_kv
    k = kv[..., :D_MODEL].reshape(B, M, XA_HEADS, XA_HDIM)
    v = kv[..., D_MODEL:].reshape(B, M, XA_HEADS, XA_HDIM)
    return k, v


def layer(x, mem_k, mem_v, lru_conv_state, lru_h0, ffn_state, lp):
    B, T, _ = x.shape
    h = rms_norm(x, lp['norm_mix'])
    z = h @ lp['w_in']
    uv = jax.nn.gelu(z[..., :2 * D_A])
    u = uv[..., :D_A]
    v = rms_norm(uv[..., D_A:], lp['g_v'])
    n = GMLP_CHUNK if T >= GMLP_CHUNK else T
    vc = v.reshape(B, T // n, n, GMLP_GROUPS, GMLP_GDIM)
    out_a = u * gmlp_spatial(vc, lp['gmlp_w'], lp['gmlp_b']).reshape(B, T, D_A)
    xr = z[..., 2 * D_A:2 * D_A + D_B]
    gate = z[..., 2 * D_A + D_B:]
    xr_c, lru_conv_new = causal_dwconv(xr, lru_conv_state, lp['lru_conv_w'], lp['lru_conv_b'])
    hseq, h_last = rglru(xr_c, lru_h0, lp['lru_wa'], lp['lru_ba'], lp['lru_wx'], lp['lru_bx'], lp['lru_lam'])
    out_b = hseq * jax.nn.gelu(gate)
    mix = jnp.concatenate([rms_norm(out_a, lp['g_a']), rms_norm(out_b, lp['g_b'])], axis=-1)
    x = x + mix @ lp['w_out']
    h = rms_norm(x, lp['norm_xa'])
    q = (h @ lp['w_q']).reshape(B, T, XA_HEADS, XA_HDIM)
    s = jnp.einsum('bthd,bmhd->bhtm', q, mem_k).astype(jnp.float32) * (XA_HDIM ** -0.5)
    p = jax.nn.softmax(s, axis=-1).astype(x.dtype)
    o = jnp.einsum('bhtm,bmhd->bthd', p, mem_v).reshape(B, T, D_MODEL)
    x = x + o @ lp['w_o']
    h = rms_norm(x, lp['norm_ffn'])
    up = h @ lp['w_up']
    up_c, ffn_new = causal_dwconv(up, ffn_state, lp['ffn_conv_w'], lp['ffn_conv_b'])
    x = x + (jax.nn.gelu(up_c[..., D_FF:]) * up_c[..., :D_FF]) @ lp['w_down']
    return x, v, lru_conv_new, h_last, ffn_new


def setup_inputs(seed: int = 0) -> dict:
    key = jax.random.key(seed)
    ks = iter(jax.random.split(key, 40))

    def nrm(shape, scale):
        return jax.random.normal(next(ks), shape, jnp.float32) * scale

    def gain(shape):
        return 1.0 + nrm(shape, 0.05)

    L = DEPTH
    a0 = jax.random.uniform(next(ks), (L, D_B), jnp.float32, 0.9, 0.999)
    return {
        'x_prompt': nrm((BATCH, SEQ, D_MODEL), 1.0),
        'x_sample': nrm((DEC_BATCH, DEC_SEQ, D_MODEL), 1.0),
        'mem_prompt': nrm((BATCH, N_MEM, D_MODEL), 1.0),
        'cache_mem_k': nrm((L, DEC_BATCH, N_MEM, XA_HEADS, XA_HDIM), 1.0),
        'cache_mem_v': nrm((L, DEC_BATCH, N_MEM, XA_HEADS, XA_HDIM), 1.0),
        'state_lru_h': nrm((L, DEC_BATCH, D_B), 0.5),
        'state_lru_conv': nrm((L, DEC_BATCH, LRU_CONV - 1, D_B), 1.0),
        'state_ffn_conv': nrm((L, DEC_BATCH, FFN_CONV - 1, 2 * D_FF), 1.0),
        'norm_mix': gain((L, D_MODEL)),
        'w_in': nrm((L, D_MODEL, D_IN), D_MODEL ** -0.5),
        'g_v': gain((L, D_A)),
        'gmlp_w': nrm((L, GMLP_GROUPS, GMLP_CHUNK, GMLP_CHUNK), GMLP_CHUNK ** -0.5),
        'gmlp_b': gain((L, GMLP_GROUPS, GMLP_CHUNK)),
        'lru_conv_w': nrm((L, LRU_CONV, D_B), LRU_CONV ** -0.5),
        'lru_conv_b': nrm((L, D_B), 0.02),
        'lru_wa': nrm((L, LRU_HEADS, LRU_HDIM, LRU_HDIM), LRU_HDIM ** -0.5),
        'lru_ba': nrm((L, D_B), 0.02),
        'lru_wx': nrm((L, LRU_HEADS, LRU_HDIM, LRU_HDIM), LRU_HDIM ** -0.5),
        'lru_bx': nrm((L, D_B), 0.02),
        'lru_lam': jnp.log(a0) - jnp.log1p(-a0),
        'g_a': gain((L, D_A)),
        'g_b': gain((L, D_B)),
        'w_out': nrm((L, D_MIX, D_MODEL), D_MIX ** -0.5),
        'norm_mem': gain((L, D_MODEL)),
        'w_kv': nrm((L, D_MODEL, 2 * D_MODEL), D_MODEL ** -0.5),
        'norm_xa': gain((L, D_MODEL)),
        'w_q': nrm((L, D_MODEL, D_MODEL), D_MODEL ** -0.5),
        'w_o': nrm((L, D_MODEL, D_MODEL), D_MODEL ** -0.5),
        'norm_ffn': gain((L, D_MODEL)),
        'w_up': nrm((L, D_MODEL, 2 * D_FF), D_MODEL ** -0.5),
        'ffn_conv_w': nrm((L, FFN_CONV, 2 * D_FF), FFN_CONV ** -0.5),
        'ffn_conv_b': nrm((L, 2 * D_FF), 0.02),
        'w_down': nrm((L, D_FF, D_MODEL), D_FF ** -0.5),
        'norm_final': gain((D_MODEL,)),
    }


def reference(x_prompt, x_sample, mem_prompt, cache_mem_k, cache_mem_v, state_lru_h, state_lru_conv,
              state_ffn_conv, norm_mix, w_in, g_v, gmlp_w, gmlp_b, lru_conv_w, lru_conv_b, lru_wa, lru_ba,
              lru_wx, lru_bx, lru_lam, g_a, g_b, w_out, norm_mem, w_kv, norm_xa, w_q, w_o, norm_ffn, w_up,
              ffn_conv_w, ffn_conv_b, w_down, norm_final):
    B = x_prompt.shape[0]
    dt = x_prompt.dtype
    xp, xs = x_prompt, x_sample
    mk_p, mv_p, h_p, c_p, f_p = [], [], [], [], []
    h_s, c_s, f_s, v_s = [], [], [], []
    for l in range(DEPTH):
        lp = {
            'norm_mix': norm_mix[l], 'w_in': w_in[l], 'g_v': g_v[l], 'gmlp_w': gmlp_w[l], 'gmlp_b': gmlp_b[l],
            'lru_conv_w': lru_conv_w[l], 'lru_conv_b': lru_conv_b[l], 'lru_wa': lru_wa[l], 'lru_ba': lru_ba[l],
            'lru_wx': lru_wx[l], 'lru_bx': lru_bx[l], 'lru_lam': lru_lam[l], 'g_a': g_a[l], 'g_b': g_b[l],
            'w_out': w_out[l], 'norm_xa': norm_xa[l], 'w_q': w_q[l], 'w_o': w_o[l], 'norm_ffn': norm_ffn[l],
            'w_up': w_up[l], 'ffn_conv_w': ffn_conv_w[l], 'ffn_conv_b': ffn_conv_b[l], 'w_down': w_down[l],
        }
        mk, mv = memory_kv(mem_prompt, norm_mem[l], w_kv[l])
        xp, _, c1, h1, f1 = layer(
            xp, mk, mv,
            jnp.zeros((B, LRU_CONV - 1, D_B), dt), jnp.zeros((B, D_B), dt),
            jnp.zeros((B, FFN_CONV - 1, 2 * D_FF), dt), lp)
        mk_p.append(mk); mv_p.append(mv); h_p.append(h1); c_p.append(c1); f_p.append(f1)
        xs, v2, c2, h2, f2 = layer(
            xs, cache_mem_k[l], cache_mem_v[l], state_lru_conv[l], state_lru_h[l], state_ffn_conv[l], lp)
        h_s.append(h2); c_s.append(c2); f_s.append(f2); v_s.append(v2)
    y_prompt = rms_norm(xp, norm_final)
    y_sample = rms_norm(xs, norm_final)
    mem_k_prompt = jnp.stack(mk_p)
    mem_v_prompt = jnp.stack(mv_p)
    lru_h_prompt = jnp.stack(h_p)
    lru_conv_prompt = jnp.stack(c_p)
    ffn_conv_prompt = jnp.stack(f_p)
    lru_h_sample = jnp.stack(h_s)
    lru_conv_sample = jnp.stack(c_s)
    ffn_conv_sample = jnp.stack(f_s)
    gmlp_v_sample = jnp.stack(v_s)
    return (y_prompt, y_sample, mem_k_prompt, mem_v_prompt, lru_h_prompt, lru_conv_prompt, ffn_conv_prompt,
            lru_h_sample, lru_conv_sample, ffn_conv_sample, gmlp_v_sample)
```

```python
import numpy as np
import concourse.bass as bass
import concourse.mybir as mybir
from concourse.bass_utils import run_bass_kernel_spmd

F32 = mybir.dt.float32
BF16 = mybir.dt.bfloat16
AF = mybir.ActivationFunctionType
ALU = mybir.AluOpType

D = 2048
DA = 1024
DB = 1024
DFF = 6144
NMEM = 256
EPS = 1e-6
NPRE = 2944
NHALO = 128
NP_ = 1024
NS = 64
NIN = NPRE + NHALO + NP_ + NS
PRE_TILES = [(0, 512), (512, 1024), (1024, 1536), (1536, 2048), (2048, 2560), (2560, 2944)]

C_GMIX, C_GXA, C_GFFN, C_GFIN, C_GMEM = 0, 16, 32, 48, 64
C_GA, C_GB = 80, 88
C_CW = 96
C_CB = 128
C_BA = 136
C_BX = 144
C_LAM = 152
C_FCW = 160
C_FCB = 448
C_MASK = 544
C_HST = 552
NCST = 584


class V:
    __slots__ = ("ap", "sp", "lo", "hi")

    def __init__(self, ap, sp, lo, hi):
        self.ap, self.sp, self.lo, self.hi = ap, sp, lo, hi


class Buf:
    def __init__(self, ap, sp, base, shape, esz):
        self.ap, self.sp, self.base, self.shape, self.esz = ap, sp, base, shape, esz
        self.rowlen = 1
        for s in shape[2:]:
            self.rowlen *= s

    def all(self):
        n = 1
        for s in self.shape[1:]:
            n *= s
        return V(self.ap, self.sp, self.base, self.base + n * self.esz)

    def __call__(self, k=None, a=None, b=None, k1=None, p0=0, p1=None):
        sh = self.shape
        p1 = sh[0] if p1 is None else p1
        if len(sh) == 2:
            a = 0 if a is None else a
            b = sh[1] if b is None else b
            return V(self.ap[p0:p1, a:b], self.sp, self.base + a * self.esz, self.base + b * self.esz)
        n = sh[2]
        a = 0 if a is None else a
        b = n if b is None else b
        if k1 is None:
            lo = self.base + (k * n + a) * self.esz
            return V(self.ap[p0:p1, k, a:b], self.sp, lo, lo + (b - a) * self.esz)
        lo = self.base + (k * n + a) * self.esz
        hi = self.base + ((k1 - 1) * n + b) * self.esz
        return V(self.ap[p0:p1, k:k1, a:b], self.sp, lo, hi)

    def raw(self, ap):
        v = self.all()
        return V(ap, v.sp, v.lo, v.hi)


class Op:
    __slots__ = ("eng", "fn", "deps", "dma_sem", "dma_val", "signal", "sigval", "idx")


class Sched:
    def __init__(self):
        self.ops = []
        self.recs = {}
        self.dma_counts = {}
        self.psum_rr = 0

    def _collect(self, op, v, write):
        if v.sp == "ps":
            bank = v.lo // 2048
            for d in self.recs.get(("psb", bank), []):
                if d.eng != op.eng:
                    op.deps.add(d)
            return
        for r in self.recs.get(v.sp, []):
            if r[0] < v.hi and v.lo < r[1]:
                if r[2] is not None:
                    op.deps.add(r[2])
                if write:
                    op.deps.update(r[3])

    def _commit(self, op, v, write):
        if v.sp == "ps":
            bank = v.lo // 2048
            recs = self.recs.setdefault(("psb", bank), [])
            self.recs[("psb", bank)] = [d for d in recs if d.eng != op.eng] + [op]
            return
        recs = self.recs.setdefault(v.sp, [])
        if write:
            keep = []
            for r in recs:
                if r[0] < v.hi and v.lo < r[1]:
                    if r[0] < v.lo:
                        keep.append([r[0], v.lo, r[2], list(r[3])])
                    if r[1] > v.hi:
                        keep.append([v.hi, r[1], r[2], list(r[3])])
                    continue
                keep.append(r)
            keep.append([v.lo, v.hi, op, []])
            self.recs[v.sp] = keep
        else:
            for r in recs:
                if r[0] < v.hi and v.lo < r[1]:
                    rd = r[3]
                    if op in rd:
                        continue
                    if op.dma_sem is None:
                        for i, x in enumerate(rd):
                            if x.dma_sem is None and x.eng == op.eng:
                                rd[i] = op
                                break
                        else:
                            rd.append(op)
                    else:
                        rd.append(op)

    def op(self, eng, fn, R=(), W=(), dma_key=None):
        o = Op()
        o.eng, o.fn, o.deps = eng, fn, set()
        o.dma_sem, o.dma_val, o.signal, o.sigval = dma_key, 0, False, 0
        o.idx = len(self.ops)
        for v in R:
            self._collect(o, v, False)
        for v in W:
            self._collect(o, v, True)
        for v in R:
            self._commit(o, v, False)
        for v in W:
            self._commit(o, v, True)
        o.deps.discard(o)
        latest = {}
        red = set()
        for d in o.deps:
            if d.dma_sem is not None:
                red.add(d)
            elif d.eng not in latest or latest[d.eng].idx < d.idx:
                latest[d.eng] = d
        red.update(latest.values())
        o.deps = red
        if dma_key is not None:
            c = self.dma_counts.get(dma_key, 0) + 16
            self.dma_counts[dma_key] = c
            o.dma_val = c
        self.ops.append(o)
        return o

    def finalize(self):
        for o in self.ops:
            for d in o.deps:
                if d.dma_sem is None:
                    if d.eng == o.eng and d.eng == "pe":
                        continue
                    d.signal = True
        cnt = {}
        for o in self.ops:
            if o.dma_sem is None and o.signal:
                cnt[o.eng] = cnt.get(o.eng, 0) + 1
                o.sigval = cnt[o.eng]


def split_cols(c0, c1):
    n = c1 - c0
    if n <= 512:
        return [(c0, c1)]
    h = (n + 1) // 2
    return [(c0, c0 + h), (c0 + h, c1)]


class _Stop(Exception):
    pass


MARKS = []


def build_program(stop=99):
    nc = bass.Bass("TRN2", target_bir_lowering=False)
    S = Sched()

    def chk(n):
        MARKS.append((n, sum(1 for o in S.ops if o.eng == "pe")))
        if n > stop:
            raise _Stop()

    def din(name, shape):
        return nc.dram_tensor(name, list(shape), F32, kind="ExternalInput").ap()

    def dout(name, shape):
        return nc.dram_tensor(name, list(shape), F32, kind="ExternalOutput").ap()

    xin = din("xin", [D, NIN])
    memT = din("memT", [D, NMEM])
    cst_d = din("cst", [128, NCST])
    cstb_d = din("cstb", [128, 1024 + 1024 + 512])
    wmT_d = din("wmT", [128, 8 * 128])
    lwa_d = din("lwa", [128, 8 * 128])
    lwx_d = din("lwx", [128, 8 * 128])
    skT_d = din("skT", [4, D, NMEM])
    sv_d = din("sv", [4, NMEM, D])
    sxr_d = din("sxr", [128, 8 * 4 * 3])
    sfc_d = din("sfc", [128, 96 * 4 * 2])
    w_in = din("w_in", [D, 4096])
    w_out = din("w_out", [D, D])
    w_kv = din("w_kv", [D, 4096])
    w_q = din("w_q", [D, D])
    w_o = din("w_o", [D, D])
    w_up = din("w_up", [D, 2 * DFF])
    w_down = din("w_down", [DFF, D])

    yT_o = dout("yT", [D, NP_ + NS])
    mkT_o = dout("mkT", [D, NMEM])
    mv_o = dout("mv", [NMEM, D])
    lruh_o = dout("lruh", [128, 8 * 5])
    lruc_o = dout("lruc", [128, 8 * 5 * 3])
    ffnc_o = dout("ffnc", [128, 96 * 5 * 2])
    gv_o = dout("gv", [NS, DA])

    kv_scr = nc.dram_tensor("kv_scr", [128, 16 * 256 + 2 * 2048], BF16).ap()

    ARENA_F32 = 53000
    ctx_arena = nc.sbuf_tensor("arena", [128, ARENA_F32], F32)
    arena = ctx_arena.__enter__()
    psum_ctx = [nc.psum_tensor("ps%d" % i, [128, 512], F32) for i in range(8)]
    psum_t = [c.__enter__() for c in psum_ctx]
    PS = [Buf(psum_t[i][:], "ps", i * 2048, [128, 512], 4) for i in range(8)]

    top = [0]

    def alloc(shape, dt):
        esz = 4 if dt == F32 else 2
        n = 1
        for s in shape[1:]:
            n *= s
        nbytes = (n * esz + 63) // 64 * 64
        off = top[0]
        top[0] += nbytes
        assert top[0] <= ARENA_F32 * 4, ("arena overflow", top[0])
        ap = arena[:, off // 4:(off + nbytes) // 4]
        if dt == BF16:
            ap = ap.bitcast(BF16)
        ap = ap[:, 0:n]
        if len(shape) == 3:
            ap = ap.rearrange("p (k n) -> p k n", k=shape[1])
        elif len(shape) == 4:
            ap = ap.rearrange("p (k b n) -> p k b n", k=shape[1], b=shape[2])
        if shape[0] < 128:
            ap = ap[0:shape[0]]
        return Buf(ap, "sb", off, list(shape), esz)

    def psum():
        b = PS[S.psum_rr % 8]
        S.psum_rr += 1
        return b

    def dma(q, out_ap, in_ap, key, R=(), W=()):
        S.op(q, lambda e: e.dma_start(out=out_ap, in_=in_ap), R, W, dma_key=key + "_" + q)

    def mm(out, lhsT, rhs, start, stop):
        bank = out.lo // 2048
        wv = V(out.ap, "ps", bank * 2048, bank * 2048 + 2048)
        S.op("pe", lambda e: e.matmul(out.ap, lhsT.ap, rhs.ap, start=start, stop=stop), [lhsT, rhs], [wv])

    def act(out, in_, func, bias=None, scale=None, accum=None, extraR=()):
        kw = {}
        R = [in_] + list(extraR)
        W = [out]
        if bias is not None:
            kw["bias"] = bias.ap if isinstance(bias, V) else bias
            if isinstance(bias, V):
                R.append(bias)
        if scale is not None:
            kw["scale"] = scale.ap if isinstance(scale, V) else scale
            if isinstance(scale, V):
                R.append(scale)
        if accum is not None:
            kw["accum_out"] = accum.ap
            W.append(accum)
        S.op("act", lambda e: e.activation(out=out.ap, in_=in_.ap, func=func, **kw), R, W)

    def ts(out, in0, s1, s2, op0, op1=None, eng="dve"):
        R = [in0]
        a1 = s1.ap if isinstance(s1, V) else s1
        a2 = s2.ap if isinstance(s2, V) else s2
        if isinstance(s1, V):
            R.append(s1)
        if isinstance(s2, V):
            R.append(s2)
        if op1 is None:
            S.op(eng, lambda e: e.tensor_scalar(out=out.ap, in0=in0.ap, scalar1=a1, scalar2=None, op0=op0), R, [out])
        else:
            S.op(eng, lambda e: e.tensor_scalar(out=out.ap, in0=in0.ap, scalar1=a1, scalar2=a2, op0=op0, op1=op1), R, [out])

    def stt(out, in0, sc, in1, op0, op1, eng="dve"):
        R = [in0, in1]
        a = sc.ap if isinstance(sc, V) else sc
        if isinstance(sc, V):
            R.append(sc)
        S.op(eng, lambda e: e.scalar_tensor_tensor(out=out.ap, in0=in0.ap, scalar=a, in1=in1.ap, op0=op0, op1=op1), R, [out])

    def tt(out, in0, in1, op, eng="dve"):
        S.op(eng, lambda e: e.tensor_tensor(out=out.ap, in0=in0.ap, in1=in1.ap, op=op), [in0, in1], [out])

    def cp(out, in_, eng="dve"):
        if eng == "act":
            S.op(eng, lambda e: e.activation(out=out.ap, in_=in_.ap, func=AF.Copy), [in_], [out])
        else:
            S.op(eng, lambda e: e.tensor_copy(out=out.ap, in_=in_.ap), [in_], [out])

    def recip(out, in_):
        S.op("dve", lambda e: e.reciprocal(out=out.ap, in_=in_.ap), [in_], [out])

    def scan(out, d0, d1, init):
        R = [d0, d1]
        a = init.ap if isinstance(init, V) else init
        if isinstance(init, V):
            R.append(init)
        S.op("dve", lambda e: e.tensor_tensor_scan(out=out.ap, data0=d0.ap, data1=d1.ap, initial=a, op0=ALU.mult, op1=ALU.add), R, [out])

    def memset(v, val, eng="dve"):
        S.op(eng, lambda e: e.memset(v.ap, val), [], [v])

    cst = alloc([128, NCST], F32)
    ones_bf = alloc([128, 128], BF16)
    wmT = alloc([128, 8, 128], BF16)
    wsT = alloc([64, 8, 64], BF16)
    lwa = alloc([128, 8, 128], BF16)
    lwx = alloc([128, 8, 128], BF16)
    drv = alloc([128, 48], F32)
    hstate = alloc([128, 8], F32)
    xr_hist = alloc([128, 8, 3], F32)
    up_hist = alloc([128, 96, 2], F32)
    xrs = alloc([128, 8, 4, 19], F32)
    fcst = alloc([128, 96, 4, 2], F32)
    fcs = alloc([128, 1, 4, 18], F32)
    o_lruh = alloc([128, 8, 5], F32)
    o_lruc = alloc([128, 8, 5, 3], F32)
    o_ffnc = alloc([128, 96, 5, 2], F32)
    mcb = alloc([128, 7, 8], F32)
    wb = [alloc([128, 16, 512], BF16), alloc([128, 16, 512], BF16)]
    wslot = [0]
    wslots = list(wb)

    def set_extra_slots(n):
        del wslots[2:]
        for _ in range(n):
            wslots.append(alloc([128, 16, 512], BF16))
    P_MARK = top[0]

    def cc(col, n=1):
        return cst(a=col, b=col + n)

    def dv(col, n=1):
        return drv(a=col, b=col + n)

    HC, C1, HBA, HBX, DEPS, DQ, DH = 0, 8, 16, 24, 32, 33, 34

    dma("sp", cst.ap, cst_d, "cst", W=[cst.all()])
    dma("sp", xrs.ap[:, :, :, 0:3], sxr_d.rearrange("p (k b n) -> p k b n", k=8, b=4), "cxrs", W=[xrs.all()])
    dma("sp", fcst.ap, sfc_d.rearrange("p (k b n) -> p k b n", k=96, b=4), "cfcs", W=[fcst.all()])
    dma("pool", wmT.ap, wmT_d.rearrange("p (g i) -> p g i", g=8), "cw", W=[wmT.all()])
    dma("pool", lwa.ap, lwa_d.rearrange("p (g i) -> p g i", g=8), "cwa", W=[lwa.all()])
    dma("pool", lwx.ap, lwx_d.rearrange("p (g i) -> p g i", g=8), "cwx", W=[lwx.all()])
    memset(wsT.all(), 0.0)
    wm3 = wmT_d.rearrange("p (g i) -> p g i", g=8)
    for b in range(4):
        dma("pool", wsT.ap[16 * b:16 * b + 16, :, 16 * b:16 * b + 16], wm3[0:16, :, 0:16], "cw2",
            W=[wsT.all()], R=[])
    memset(wmT.raw(wmT.ap[64:128, :, 0:64]), 0.0)
    memset(ones_bf.all(), 1.0)
    memset(dv(DEPS), EPS)
    memset(dv(DQ), 0.25)
    memset(hstate.all(), 0.0)
    memset(xr_hist.all(), 0.0)
    memset(up_hist.all(), 0.0)
    act(dv(HC, 8), cc(C_LAM, 8), AF.Exp, scale=-1.0)
    act(dv(C1, 8), dv(HC, 8), AF.Ln, bias=1.0)
    ts(dv(HC, 8), dv(C1, 8), -4.0, None, ALU.mult)
    ts(dv(C1, 8), dv(C1, 8), -8.0, None, ALU.mult)
    ts(dv(HBA, 8), cc(C_BA, 8), 0.5, None, ALU.mult)
    ts(dv(HBX, 8), cc(C_BX, 8), 0.5, None, ALU.mult)
    for m_ in range(7):
        ts(mcb(m_), cc(C_CB, 8), cc(C_MASK + m_), None, ALU.mult)

    def load_w(src_ap, kch=16, ncol=512):
        si_ = wslot[0] % len(wslots)
        slot = wslots[si_]
        wslot[0] += 1
        dst = slot.ap if (kch == 16 and ncol == 512) else \
            slot.ap.rearrange("p k n -> p (k n)")[:, 0:kch * ncol].rearrange("p (k n) -> p k n", k=kch)
        dma("pool", dst, src_ap.rearrange("(kc p) n -> p kc n", p=128), "w%d" % si_,
            W=[slot.all()])
        return Buf(dst, "sb", slot.base, [128, kch, ncol], 2)

    def rmsnorm_fm(xb, gcol, outb, c0, c1, nk, dim, tmp_sq, rstd, sqb=None):
        sqb = outb if sqb is None else sqb
        for ti, (a, b) in enumerate(split_cols(c0, c1)):
            n = b - a
            ps = psum()
            act(sqb(0, a, b, k1=nk), xb(0, a, b, k1=nk), AF.Square)
            for k in range(nk):
                mm(ps(a=0, b=n), ones_bf.all(), sqb(k, a, b), k == 0, k == nk - 1)
            act(rstd(ti, 0, n), ps(a=0, b=n), AF.Sqrt, bias=dv(DEPS), scale=1.0 / dim)
            recip(rstd(ti, 0, n), rstd(ti, 0, n))
            for k in range(nk):
                stt(outb(k, a, b), xb(k, a, b), cc(gcol + k), rstd(ti, 0, n), ALU.mult, ALU.mult)

    def lru_heads(heads, hsrc, wsrc_fn, ncols, prompt_n, mask_col, mask_n, L, sample, rstd=None):
        tiles = split_cols(0, ncols)
        xb, xc, xcbf = L["xrbuf"], L["xc"], L["xcbf"]
        tha, thx, A_, A2 = L["tha"], L["thx"], L["a"], L["a2"]
        hs = L["hseq"]
        n = prompt_n
        H = list(enumerate(heads))
        for kl, k in H:
            wbuf, wc0 = wsrc_fn(kl)
            cp(xb(kl, 0, 3), xr_hist(k), eng="act")
            for (a, b) in tiles:
                ps = psum()
                for kc in range(16):
                    mm(ps(a=0, b=b - a), wbuf(kc, wc0, wc0 + 128), hsrc(kc, a, b), kc == 0, kc == 15)
                pa, pb = a, min(b, prompt_n)
                if pb > pa:
                    if rstd is None:
                        cp(xb(kl, 3 + pa, 3 + pb), ps(a=0, b=pb - pa), eng="act")
                    else:
                        tt(xb(kl, 3 + pa, 3 + pb), ps(a=0, b=pb - pa), rstd(a=pa, b=pb), ALU.mult)
                if sample and b > prompt_n:
                    sa = max(a, prompt_n)
                    assert sa == prompt_n and b == prompt_n + 64
                    src = ps.ap[:, sa - a:b - a].rearrange("p (b t) -> p b t", b=4)
                    S.op("act", lambda e, o=xrs.ap[:, k, :, 3:19], i=src: e.activation(out=o, in_=i, func=AF.Copy),
                         [ps.all()], [xrs.all()])
        yield
        for kl, k in H:
            if mask_n > 0:
                ts(xc(kl, 0, mask_n), xb(kl, 0, mask_n), cc(C_CW + 4 * k), mcb(mask_col, k, k + 1), ALU.mult, ALU.add)
            if n > mask_n:
                ts(xc(kl, mask_n, n), xb(kl, mask_n, n), cc(C_CW + 4 * k), cc(C_CB + k), ALU.mult, ALU.add)
        for j in range(1, 4):
            for kl, k in H:
                stt(xc(kl, 0, n), xb(kl, j, j + n), cc(C_CW + 4 * k + j), xc(kl, 0, n), ALU.mult, ALU.add)
        for kl, k in H:
            cp(xr_hist(k), xb(kl, n, n + 3), eng="act")
        if sample:
            for kl, k in H:
                xo = xc.ap[:, kl, n:n + 64].rearrange("p (b t) -> p b t", b=4)
                xov = xc(kl, n, n + 64)
                S.op("dve", lambda e, o=xo, i=xrs.ap[:, k, :, 0:16], s1=cc(C_CW + 4 * k).ap, s2=cc(C_CB + k).ap:
                     e.tensor_scalar(out=o, in0=i, scalar1=s1, scalar2=s2, op0=ALU.mult, op1=ALU.add),
                     [xrs.all(), cst.all()], [xov])
                for j in range(1, 4):
                    S.op("dve", lambda e, o=xo, i=xrs.ap[:, k, :, j:j + 16], s1=cc(C_CW + 4 * k + j).ap:
                         e.scalar_tensor_tensor(out=o, in0=i, scalar=s1, in1=o, op0=ALU.mult, op1=ALU.add),
                         [xrs.all(), cst.all(), xov], [xov])
                S.op("act", lambda e, o=o_lruc.ap[:, k, 1:5, :], i=xrs.ap[:, k, :, 16:19]:
                     e.activation(out=o, in_=i, func=AF.Copy), [xrs.all()], [o_lruc.all()])
        for kl, k in H:
            cp(xcbf(kl, 0, ncols), xc(kl, 0, ncols), eng="act")
        yield
        for (a, b) in tiles:
            nn = b - a
            pp = {}
            for kl, k in H:
                psa, psx = psum(), psum()
                mm(psa(a=0, b=nn), lwa(k), xcbf(kl, a, b), True, True)
                mm(psx(a=0, b=nn), lwx(k), xcbf(kl, a, b), True, True)
                pp[kl] = (psa, psx)
                if kl % 2 == 1 or kl == len(heads) - 1:
                    for kk in ([kl - 1, kl] if kl % 2 == 1 else [kl]):
                        pa_, px_ = pp[kk]
                        kg = heads[kk]
                        act(tha(kk, 0, nn), pa_(a=0, b=nn), AF.Tanh, bias=dv(HBA + kg), scale=0.5)
                        act(thx(kk, 0, nn), px_(a=0, b=nn), AF.Tanh, bias=dv(HBX + kg), scale=0.5)
            for kl, k in H:
                act(A_(kl, a, b), tha(kl, 0, nn), AF.Exp, bias=dv(HC + k), scale=dv(HC + k))
                act(A2(kl, a, b), tha(kl, 0, nn), AF.Exp, bias=dv(C1 + k), scale=dv(C1 + k))
            for kl, k in H:
                stt(xc(kl, a, b), thx(kl, 0, nn), 1.0, xc(kl, a, b), ALU.add, ALU.mult)
        yield
        for kl, k in H:
            act(A2(kl, 0, ncols), A2(kl, 0, ncols), AF.Sqrt, bias=dv(DQ), scale=-0.25)
        yield
        for kl, k in H:
            tt(xc(kl, 0, ncols), xc(kl, 0, ncols), A2(kl, 0, ncols), ALU.mult)
        for kl, k in H:
            hk = L["hidx"](k, kl)
            scan(hs(hk, 0, prompt_n), A_(kl, 0, prompt_n), xc(kl, 0, prompt_n), hstate(a=k, b=k + 1))
        for kl, k in H:
            hk = L["hidx"](k, kl)
            cp(hstate(a=k, b=k + 1), hs(hk, prompt_n - 1, prompt_n), eng="act")
        if sample:
            for kl, k in H:
                hk = L["hidx"](k, kl)
                for b in range(4):
                    c0 = prompt_n + 16 * b
                    scan(hs(hk, c0, c0 + 16), A_(kl, c0, c0 + 16), xc(kl, c0, c0 + 16),
                         cc(C_HST + 4 * k + b))
                    cp(o_lruh.raw(o_lruh.ap[:, k, 1 + b:2 + b]), hs(hk, c0 + 15, c0 + 16), eng="act")

    def run_gens(gens):
        gens = list(gens)
        while gens:
            nxt = []
            for g in gens:
                try:
                    next(g)
                    nxt.append(g)
                except StopIteration:
                    pass
            gens = nxt

    def phases():
        chk(1)
        phase_kv()
        chk(2)
        phase_pre()
        chk(3)
        supertile(0)
        chk(10)
        supertile(1)

    KV_D = V(kv_scr, "dram_kv", 0, 1)

    def phase_kv():
      top[0] = P_MARK
      if True:
        mem_f = alloc([128, 16, NMEM], F32)
        hmem = alloc([128, 16, NMEM], BF16)
        sq_t = None
        rs_t = alloc([128, 2, 512], F32)
        kt_bf = alloc([128, 16, NMEM], BF16)
        v_bf = alloc([128, 2, D], BF16)
        stg = [alloc([128, 512], F32), alloc([128, 512], F32)]
        import os
        KVL = int(os.environ.get("KV_LEVEL", "9"))
        dma("sp", mem_f.ap, memT.rearrange("(kc p) m -> p kc m", p=128), "memf", W=[mem_f.all()])
        if KVL < 1:
            return
        rmsnorm_fm(mem_f, C_GMEM, hmem, 0, NMEM, 16, D, sq_t, rs_t)
        if KVL < 2:
            return
        si = 0
        for t in range(4 if KVL >= 3 else 1):
            w = load_w(w_kv[:, 512 * t:512 * t + 512])
            for m in range(4):
                nch = 4 * t + m
                ps = psum()
                for kc in range(16):
                    mm(ps(a=0, b=NMEM), w(kc, m * 128, m * 128 + 128), hmem(kc), kc == 0, kc == 15)
                sg = stg[si % 2]
                si += 1
                cp(sg(a=0, b=NMEM), ps(a=0, b=NMEM), eng="act")
                cp(kt_bf(nch), ps(a=0, b=NMEM))
                dma("sp", mkT_o[nch * 128:(nch + 1) * 128, :], sg.ap[:, 0:NMEM], "stg%d" % ((si - 1) % 2), R=[sg.all()])
        if KVL < 4:
            return
        for t in range(4):
            w = load_w(w_kv[:, 2048 + 512 * t:2048 + 512 * t + 512])
            for mc in range(2):
                ps = psum()
                for kc in range(16):
                    mm(ps.all(), hmem(kc, mc * 128, mc * 128 + 128), w(kc), kc == 0, kc == 15)
                sg = stg[si % 2]
                si += 1
                cp(sg.all(), ps.all(), eng="act")
                cp(v_bf(mc, 512 * t, 512 * t + 512), ps.all())
                dma("sp", mv_o[mc * 128:(mc + 1) * 128, 512 * t:512 * t + 512], sg.ap, "stg%d" % ((si - 1) % 2), R=[sg.all()])

        import os
        if os.environ.get("KV_NOSCR"):
            return
        dma("sp", kv_scr[:, 0:4096], kt_bf.ap.rearrange("p k n -> p (k n)"), "kvs", R=[kt_bf.all()], W=[KV_D])
        dma("sp", kv_scr[:, 4096:8192], v_bf.ap.rearrange("p k n -> p (k n)"), "kvs", R=[v_bf.all()], W=[KV_D])

    def phase_pre():
      set_extra_slots(0)
      if True:
        top[0] = P_MARK
        wxr = Buf(arena[:, wb[0].base // 4:(wb[0].base + 32768) // 4].bitcast(BF16).rearrange("p (k n) -> p k n", k=16),
                  "sb", wb[0].base, [128, 16, 1024], 2)
        assert wb[1].base == wb[0].base + 16384
        xt = [alloc([128, 16, 512], BF16), alloc([128, 16, 512], BF16)]
        h1p = alloc([128, 16, 512], BF16)
        sq_t = None
        rs_t = alloc([128, 2, 512], F32)
        def mkL():
            d = {
                "xrbuf": alloc([128, 4, 3 + 512], F32), "xc": alloc([128, 4, 512], F32), "xcbf": alloc([128, 4, 512], BF16),
                "tha": alloc([128, 4, 512], F32), "thx": alloc([128, 4, 512], F32),
                "a": alloc([128, 4, 512], F32),
                "hidx": (lambda k, kl: kl),
            }
            d["hseq"] = d["xc"]
            d["a2"] = Buf(d["xrbuf"].ap[:, :, 0:512], "sb", d["xrbuf"].base, [128, 4, 515], 4)
            return d
        Ls = [mkL(), mkL()]
        for hlf in range(2):
            dma("pool", wxr.ap[:, :, 512 * hlf:512 * hlf + 512],
                w_in[:, 2048 + 512 * hlf:2048 + 512 * hlf + 512].rearrange("(kc p) n -> p kc n", p=128), "wxr",
                W=[wxr.all()])
        for kc in range(16):
            ts(wxr(kc), wxr(kc), cc(C_GMIX + kc), None, ALU.mult)
        def step(g):
            if g is None:
                return
            try:
                next(g)
            except StopIteration:
                pass

        prev = None
        for ti, (c0, c1) in enumerate(PRE_TILES):
            n = c1 - c0
            xb = xt[ti % 2]
            dma("pool", xb.ap[:, :, 0:n], xin[:, c0:c1].rearrange("(kc p) n -> p kc n", p=128), "x%d" % (ti % 2),
                W=[xb.all()])
            ps = psum()
            act(h1p(0, 0, n, k1=16), xb(0, 0, n, k1=16), AF.Square)
            for k in range(16):
                mm(ps(a=0, b=n), ones_bf.all(), h1p(k, 0, n), k == 0, k == 15)
            rstd = Buf(rs_t.ap[:, ti % 2, :], "sb", rs_t.base + (ti % 2) * 512 * 4, [128, 512], 4)
            act(rstd(a=0, b=n), ps(a=0, b=n), AF.Sqrt, bias=dv(DEPS), scale=1.0 / D)
            recip(rstd(a=0, b=n), rstd(a=0, b=n))
            G = [lru_heads([4 * hg + i for i in range(4)], xb, lambda kl, hg=hg: (wxr, 512 * hg + 128 * kl),
                           n, n, ti, n, Ls[hg], False, rstd=rstd) for hg in range(2)]
            step(G[0])
            step(prev)
            step(G[0])
            step(prev)
            step(prev)
            step(G[1])
            step(G[0])
            step(G[1])
            step(G[0])
            step(G[0])
            prev = G[1]
        step(prev)
        step(prev)
        step(prev)

    def supertile(st):
        top[0] = P_MARK
        set_extra_slots(0)
        if st == 0:
            xc0, ncm, offp, prompt_n, sample = NPRE, NHALO + 512, NHALO - 2, NHALO + 512, False
            ycol0, ysrc0 = 0, NHALO
        else:
            xc0, ncm, offp, prompt_n, sample = NPRE + NHALO + 512, 512 + NS, 0, 512, True
            ycol0, ysrc0 = 512, 0
        npost = ncm - offp
        xT = alloc([128, 16, 640], F32)
        hb = alloc([128, 16, 640], BF16)
        mixb = alloc([128, 16, 640], BF16)
        sq_t = None
        rs_t = alloc([128, 2, 512], F32)
        T_MARK = top[0]
        dma("sp", xT.ap[:, :, 0:ncm], xin[:, xc0:xc0 + ncm].rearrange("(kc p) n -> p kc n", p=128), "xT",
            W=[xT.all()])
        rmsnorm_fm(xT, C_GMIX, hb, 0, ncm, 16, D, sq_t, rs_t)
        mtiles = split_cols(0, ncm)

        cstb = alloc([128, 2560], F32)
        dma("sp", cstb.ap, cstb_d, "cstb", W=[cstb.all()])
        X = alloc([128, 8, 640], F32)
        vtok = alloc([128, 6, 1024], BF16)
        ssv = alloc([128, 8], F32)
        junk = alloc([128, 1024], BF16)
        utmp2 = [alloc([128, 640], F32), alloc([128, 640], F32)]
        utmp = utmp2[0]
        gvs = alloc([64, 1024], F32)
        nchunk = prompt_n // 128
        chunks = [(c * 128, 128) for c in range(nchunk)] + ([(prompt_n, 64)] if sample else [])
        vg_ap = X.ap.rearrange("p k n -> p (k n)")[:, 0:5 * 1024].rearrange("p (c n) -> p c n", c=5)
        vgS = utmp
        vgs_buf = alloc([128, 1024], F32)
        wv0 = load_w(w_in[:, 1024:1536])
        wv1 = load_w(w_in[:, 1536:2048])
        memset(ssv.all(), 0.0)
        for ci, (t0, tn) in enumerate(chunks):
            for hlf, wv in enumerate((wv0, wv1)):
                ps = psum()
                for kc in range(16):
                    mm(ps(a=0, b=512, p1=tn), hb(kc, t0, t0 + tn), wv(kc), kc == 0, kc == 15)
                if tn == 128:
                    dst = X.raw(vg_ap[:, ci, 512 * hlf:512 * hlf + 512])
                else:
                    dst = vgs_buf(a=512 * hlf, b=512 * hlf + 512, p1=tn)
                act(dst, ps(a=0, b=512, p1=tn), AF.Gelu_apprx_tanh)
            src = X.raw(vg_ap[:, ci, :]) if tn == 128 else vgs_buf(p1=tn)
            act(junk(p1=tn), src, AF.Square, accum=ssv(a=ci, b=ci + 1, p1=tn))
        act(ssv.all(), ssv.all(), AF.Sqrt, bias=dv(DEPS), scale=1.0 / DA)
        recip(ssv.all(), ssv.all())
        gvb = cstb(a=0, b=1024)
        for ci, (t0, tn) in enumerate(chunks):
            src = X.raw(vg_ap[:, ci, :]) if tn == 128 else vgs_buf(p1=tn)
            stt(vtok(ci, p1=tn), src, ssv(a=ci, b=ci + 1, p1=tn), cstb(a=0, b=1024, p1=tn), ALU.mult, ALU.mult)
            if tn == 64:
                stt(gvs.all(), src, ssv(a=ci, b=ci + 1, p1=tn), cstb(a=0, b=1024, p1=tn), ALU.mult, ALU.mult)
                dma("sp", gv_o, gvs.ap, "gvs", R=[gvs.all()])
        for ci, (t0, tn) in enumerate(chunks):
            for gh in range(2):
                ps = psum()
                for gl in range(4):
                    g = 4 * gh + gl
                    if tn == 128:
                        mm(ps(a=128 * gl, b=128 * gl + 128), vtok(ci, 128 * g, 128 * g + 128), wmT(g), True, True)
                    else:
                        mm(ps(a=128 * gl, b=128 * gl + 64), vtok(ci, 128 * g, 128 * g + 128, p1=64),
                           wsT.raw(wsT.ap[:, g, :]), True, True)
                for gl in range(4):
                    g = 4 * gh + gl
                    if tn == 128:
                        tt(X(g, t0, t0 + 128), ps(a=128 * gl, b=128 * gl + 128), cstb(a=1024 + 128 * g, b=1024 + 128 * g + 128), ALU.add)
                    else:
                        tt(X(g, t0, t0 + 64), ps(a=128 * gl, b=128 * gl + 64), cstb(a=2048 + 64 * g, b=2048 + 64 * g + 64), ALU.add)
        for ug in range(2):
            w = load_w(w_in[:, 512 * ug:512 * ug + 512])
            for m in range(4):
                g = 4 * ug + m
                for (a, b) in mtiles:
                    ps = psum()
                    for kc in range(16):
                        mm(ps(a=0, b=b - a), w(kc, 128 * m, 128 * m + 128), hb(kc, a, b), kc == 0, kc == 15)
                    ut = utmp2[g % 2]
                    act(ut(a=a, b=b), ps(a=0, b=b - a), AF.Gelu_apprx_tanh)
                    tt(X(g, a, b), X(g, a, b), ut(a=a, b=b), ALU.mult)
        rmsnorm_fm(X, C_GA, Buf(mixb.ap[:, 0:8, :], "sb", mixb.base, [128, 8, 640], 2), 0, ncm, 8, DA, sq_t, rs_t)

        chk(4 + 10 * st)
        top[0] = T_MARK
        set_extra_slots(0)
        L = {
            "xrbuf": alloc([128, 4, 3 + 640], F32), "xc": alloc([128, 4, 640], F32), "xcbf": alloc([128, 4, 640], BF16),
            "tha": alloc([128, 4, 320], F32), "thx": alloc([128, 4, 320], F32),
            "a": alloc([128, 4, 640], F32), "hseq": alloc([128, 8, 640], F32),
            "hidx": (lambda k, kl: k),
        }
        L["a2"] = Buf(L["xrbuf"].ap[:, :, 0:640], "sb", L["xrbuf"].base, [128, 4, 643], 4)
        _tf = L["tha"].ap.rearrange("p k n -> p (k n)")
        gtmp2 = [Buf(_tf[:, 0:640], "sb", L["tha"].base, [128, 640], 4),
                 Buf(_tf[:, 640:1280], "sb", L["tha"].base + 640 * 4, [128, 640], 4)]
        for hg in range(2):
            w = load_w(w_in[:, 2048 + 512 * hg:2048 + 512 * hg + 512])
            run_gens([lru_heads([4 * hg + i for i in range(4)], hb, lambda kl, w=w: (w, 128 * kl),
                                ncm, prompt_n, 6, (NHALO if st == 0 else 0), L, sample)])
            w = load_w(w_in[:, 3072 + 512 * hg:3072 + 512 * hg + 512])
            for m in range(4):
                k = 4 * hg + m
                for (a, b) in mtiles:
                    ps = psum()
                    for kc in range(16):
                        mm(ps(a=0, b=b - a), w(kc, 128 * m, 128 * m + 128), hb(kc, a, b), kc == 0, kc == 15)
                    gtmp = gtmp2[m % 2]
                    act(gtmp(a=a, b=b), ps(a=0, b=b - a), AF.Gelu_apprx_tanh)
                    tt(L["hseq"](k, a, b), L["hseq"](k, a, b), gtmp(a=a, b=b), ALU.mult)
        if st == 1:
            for k in range(8):
                cp(o_lruh.raw(o_lruh.ap[:, k, 0:1]), hstate(a=k, b=k + 1), eng="act")
                cp(o_lruc.raw(o_lruc.ap[:, k, 0, :]), xr_hist(k), eng="act")
        rmsnorm_fm(L["hseq"], C_GB, Buf(mixb.ap[:, 8:16, :], "sb", mixb.base + 8 * 640 * 2, [128, 8, 640], 2),
                   0, ncm, 8, DB, sq_t, rs_t)

        chk(5 + 10 * st)
        top[0] = T_MARK
        set_extra_slots(2)
        ptiles = split_cols(offp, ncm)

        def proj_res(wsrc):
            for t in range(4):
                w = load_w(wsrc[:, 512 * t:512 * t + 512])
                for m in range(4):
                    oc = 4 * t + m
                    for (a, b) in ptiles:
                        ps = psum()
                        for kc in range(16):
                            mm(ps(a=0, b=b - a), w(kc, 128 * m, 128 * m + 128), mixb(kc, a, b), kc == 0, kc == 15)
                        tt(xT(oc, a, b), xT(oc, a, b), ps(a=0, b=b - a), ALU.add)

        proj_res(w_out)

        chk(6 + 10 * st)
        top[0] = T_MARK
        set_extra_slots(0)
        rmsnorm_fm(xT, C_GXA, hb, offp, ncm, 16, D, sq_t, rs_t)
        qT = alloc([128, 16, 640], BF16)
        ktb2 = [alloc([128, 16, NMEM], BF16), alloc([128, 16, NMEM], BF16)]
        vb2 = [alloc([128, 2, D], BF16), alloc([128, 2, D], BF16)]
        ktb, vb = ktb2[0], vb2[0]
        pT2 = [alloc([128, 2, 512], BF16), alloc([128, 2, 512], BF16)]
        rden2 = [alloc([128, 512], F32), alloc([128, 512], F32)]
        att_i = [0]
        kv_jobs = ([("s", bi) for bi in range(4)] if sample else []) + [("p", 0)]

        def kv_load(job, slot):
            kb_, vb_ = ktb2[slot], vb2[slot]
            if job[0] == "s":
                bi = job[1]
                dma("pool", kb_.ap, skT_d[bi].rearrange("(kc p) m -> p kc m", p=128), "kvlk%d" % slot, W=[kb_.all()])
                dma("pool", vb_.ap, sv_d[bi].rearrange("(mc p) n -> p mc n", p=128), "kvlv%d" % slot, W=[vb_.all()])
            else:
                dma("sp", kb_.ap.rearrange("p k n -> p (k n)"), kv_scr[:, 0:4096], "kvlk%d" % slot, R=[KV_D], W=[kb_.all()])
                dma("sp", vb_.ap.rearrange("p k n -> p (k n)"), kv_scr[:, 4096:8192], "kvlv%d" % slot, R=[KV_D], W=[vb_.all()])

        def kv_cols(job):
            if job[0] == "s":
                return [(prompt_n + 16 * job[1], prompt_n + 16 * job[1] + 16)]
            return split_cols(offp, prompt_n)

        for i in range(min(2, len(kv_jobs))):
            kv_load(kv_jobs[i], i)
        for t in range(4):
            w = load_w(w_q[:, 512 * t:512 * t + 512])
            for m in range(4):
                oc = 4 * t + m
                for (a, b) in ptiles:
                    ps = psum()
                    for kc in range(16):
                        mm(ps(a=0, b=b - a), w(kc, 128 * m, 128 * m + 128), hb(kc, a, b), kc == 0, kc == 15)
                    cp(qT(oc, a, b), ps(a=0, b=b - a), eng="act")

        def attend(cols, ktb=ktb, vb=vb):
            for hd in range(4):
                for (a, b) in cols:
                    n = b - a
                    pT, rden = pT2[att_i[0] % 2], rden2[att_i[0] % 2]
                    att_i[0] += 1
                    for mc in range(2):
                        ps = psum()
                        for dc in range(4):
                            mm(ps(a=0, b=n), ktb(4 * hd + dc, 128 * mc, 128 * mc + 128), qT(4 * hd + dc, a, b),
                               dc == 0, dc == 3)
                        act(pT(mc, 0, n), ps(a=0, b=n), AF.Exp, scale=512.0 ** -0.5)
                    ps = psum()
                    for mc in range(2):
                        mm(ps(a=0, b=n), ones_bf.all(), pT(mc, 0, n), mc == 0, mc == 1)
                    recip(rden(a=0, b=n), ps(a=0, b=n))
                    for dvc in range(4):
                        ps = psum()
                        for mc in range(2):
                            mm(ps(a=0, b=n), vb(mc, 512 * hd + 128 * dvc, 512 * hd + 128 * dvc + 128), pT(mc, 0, n),
                               mc == 0, mc == 1)
                        tt(mixb(4 * hd + dvc, a, b), ps(a=0, b=n), rden(a=0, b=n), ALU.mult)

        for i, job in enumerate(kv_jobs):
            attend(kv_cols(job), ktb2[i % 2], vb2[i % 2])
            if i + 2 < len(kv_jobs):
                kv_load(kv_jobs[i + 2], i % 2)
        proj_res(w_o)

        chk(7 + 10 * st)
        top[0] = mixb.base
        set_extra_slots(0)
        sq_t = None
        hid = alloc([128, 48, 576], BF16)
        ub2 = [alloc([128, 2, 592], F32), alloc([128, 2, 592], F32)]
        vc2 = [alloc([128, 592], F32), alloc([128, 592], F32)]
        gc2 = [alloc([128, 592], F32), alloc([128, 592], F32)]
        assert vc2[1].base == vc2[0].base + 592 * 4
        rs_t = Buf(arena[:, vc2[0].base // 4:vc2[0].base // 4 + 1184].rearrange("p (k n) -> p k n", k=2), "sb",
                   vc2[0].base, [128, 2, 592], 4)
        rmsnorm_fm(xT, C_GFFN, hb, offp, ncm, 16, D, sq_t, rs_t)
        np_prompt = prompt_n - offp
        set_extra_slots(1)
        vc4 = vc2 + gc2
        gcx = [alloc([128, 592], F32), alloc([128, 592], F32)]

        n_p = np_prompt
        n_c = n_p + (72 if sample else 0)

        def s3(buf_ap, c0):
            return buf_ap[:, c0:c0 + 72].rearrange("p (b t) -> p b t", b=4)

        def up_chunk(j, q, vg, w, ub, dst):
            ch = 48 * vg + 4 * j + q
            ubv = ub.ap[:, vg, :]
            cp(ub(vg, 0, 2), up_hist(ch), eng="act")
            if sample:
                S.op("act", lambda e, o=s3(ubv, 2 + n_p)[:, :, 0:2], i=fcst.ap[:, ch, :, :]:
                     e.activation(out=o, in_=i, func=AF.Copy), [fcst.all()], [ub(vg, 2 + n_p, 2 + n_p + 72)])
            for (a, b) in ptiles:
                ps = psum()
                for kc in range(16):
                    mm(ps(a=0, b=b - a), w(kc, 128 * q, 128 * q + 128), hb(kc, a, b), kc == 0, kc == 15)
                pa, pb = a, min(b, prompt_n)
                cp(ub(vg, 2 + pa - offp, 2 + pb - offp), ps(a=0, b=pb - pa), eng="act")
                if sample and b > prompt_n:
                    src = ps.ap[:, prompt_n - a:b - a].rearrange("p (b t) -> p b t", b=4)
                    S.op("act", lambda e, o=s3(ubv, 2 + n_p)[:, :, 2:18], i=src: e.activation(out=o, in_=i, func=AF.Copy),
                         [ps.all()], [ub(vg, 2 + n_p, 2 + n_p + 72)])
            if st == 0:
                ts(ub(vg, 2, 4), ub(vg, 2, 4), cc(C_MASK + 6), None, ALU.mult)
            cp(up_hist(ch), ub(vg, n_p, n_p + 2), eng="act")
            if st == 1:
                cp(o_ffnc.raw(o_ffnc.ap[:, ch, 0, :]), ub(vg, n_p, n_p + 2), eng="act")
            ts(dst(a=0, b=n_c), ub(vg, 0, n_c), cc(C_FCW + 3 * ch), cc(C_FCB + ch), ALU.mult, ALU.add)
            for jj in range(1, 3):
                stt(dst(a=0, b=n_c), ub(vg, jj, jj + n_c), cc(C_FCW + 3 * ch + jj), dst(a=0, b=n_c), ALU.mult, ALU.add)
            if sample:
                S.op("act", lambda e, o=o_ffnc.ap[:, ch, 1:5, :], i=s3(ubv, 2 + n_p)[:, :, 16:18]:
                     e.activation(out=o, in_=i, func=AF.Copy), [ub(vg, 2 + n_p, 2 + n_p + 72)], [o_ffnc.all()])

        for j in range(12):
            wv = load_w(w_up[:, 512 * j:512 * j + 512])
            wg = load_w(w_up[:, DFF + 512 * j:DFF + 512 * j + 512])
            for q in range(4):
                up_chunk(j, q, 0, wv, ub2[q % 2], vc4[q])
            for q in range(4):
                gc = gcx[q % 2]
                up_chunk(j, q, 1, wg, ub2[q % 2], gc)
                act(gc(a=0, b=n_c), gc(a=0, b=n_c), AF.Gelu_apprx_tanh)
                vq = vc4[q]
                tt(hid(4 * j + q, 0, n_p), vq(a=0, b=n_p), gc(a=0, b=n_p), ALU.mult)
                if sample:
                    ho = hid.ap[:, 4 * j + q, n_p:n_p + 64].rearrange("p (b t) -> p b t", b=4)
                    S.op("dve", lambda e, o=ho, i0_=s3(vq.ap, n_p + 2)[:, :, 0:16], i1_=s3(gc.ap, n_p + 2)[:, :, 0:16]:
                         e.tensor_tensor(out=o, in0=i0_, in1=i1_, op=ALU.mult),
                         [vq(a=n_p, b=n_p + 74), gc(a=n_p, b=n_p + 74)], [hid(4 * j + q, n_p, n_p + 64)])
        chk(8 + 10 * st)
        for mp in range(8):
            pss = {}
            for kh in range(2):
                w = load_w(w_down[3072 * kh:3072 * kh + 3072, 256 * mp:256 * mp + 256], kch=24, ncol=256)
                for m in range(2):
                    for ti, (a, b) in enumerate(ptiles):
                        if kh == 0:
                            pss[(m, ti)] = psum()
                        ps = pss[(m, ti)]
                        for kc in range(24):
                            mm(ps(a=0, b=b - a), w(kc, 128 * m, 128 * m + 128), hid(24 * kh + kc, a - offp, b - offp),
                               kh == 0 and kc == 0, kh == 1 and kc == 23)
            for m in range(2):
                oc = 2 * mp + m
                for ti, (a, b) in enumerate(ptiles):
                    tt(xT(oc, a, b), xT(oc, a, b), pss[(m, ti)](a=0, b=b - a), ALU.add)
        chk(9 + 10 * st)
        top[0] = mixb.base
        set_extra_slots(0)
        sq_t = None
        rs_t = alloc([128, 2, 512], F32)
        yb = alloc([128, 16, 640], F32)
        rmsnorm_fm(xT, C_GFIN, yb, offp, ncm, 16, D, sq_t, rs_t, sqb=hb)
        nout = ncm - ysrc0
        dma("sp", yT_o[:, ycol0:ycol0 + nout].rearrange("(kc p) n -> p kc n", p=128), yb.ap[:, :, ysrc0:ncm], "yout",
            R=[yb.all()])

    try:
        phases()
    except _Stop:
        pass
    dma("sp", lruh_o, o_lruh.ap.rearrange("p k b -> p (k b)"), "fin", R=[o_lruh.all()])
    dma("sp", lruc_o, o_lruc.ap.rearrange("p k b n -> p (k b n)"), "fin", R=[o_lruc.all()])
    dma("sp", ffnc_o, o_ffnc.ap.rearrange("p k b n -> p (k b n)"), "fin", R=[o_ffnc.all()])

    S.finalize()
    ENG = ["pe", "act", "dve", "pool", "sp"]
    sem_ctx = {}
    sems = {}
    for e in ENG:
        sem_ctx[e] = nc.semaphore("s_" + e)
        sems[e] = sem_ctx[e].__enter__()
    for k in S.dma_counts:
        sem_ctx[k] = nc.semaphore("d_" + k)
        sems[k] = sem_ctx[k].__enter__()

    def emit(engname, handle):
        waited = {}
        for o in S.ops:
            if o.eng != engname:
                continue
            need = {}
            for d in o.deps:
                if d.dma_sem is not None:
                    key, val = d.dma_sem, d.dma_val
                else:
                    if d.eng == engname and engname == "pe":
                        continue
                    key, val = d.eng, d.sigval
                if val > need.get(key, 0):
                    need[key] = val
            for key, val in need.items():
                if waited.get(key, 0) >= val:
                    continue
                waited[key] = val
                handle.wait_ge(sems[key], val)
            ins = o.fn(handle)
            if o.dma_sem is not None:
                ins.then_inc(sems[o.dma_sem], 16)
            elif o.signal:
                ins.then_inc(sems[engname], 1)
        if engname == "sp":
            for k, c in S.dma_counts.items():
                if waited.get(k, 0) < c:
                    handle.wait_ge(sems[k], c)

    with nc.Block() as block:
        @block.tensor
        def _(t):
            emit("pe", t)

        @block.scalar
        def _(s):
            emit("act", s)

        @block.vector
        def _(v):
            emit("dve", v)

        @block.gpsimd
        def _(g):
            emit("pool", g)

        @block.sync
        def _(sy):
            emit("sp", sy)
    return nc


_NC = None


def _fm(v, k):
    return np.ascontiguousarray(np.asarray(v, np.float32).reshape(k, 128).T)


def kernel(x_prompt, x_sample, mem_prompt, cache_mem_k, cache_mem_v, state_lru_h, state_lru_conv,
           state_ffn_conv, norm_mix, w_in, g_v, gmlp_w, gmlp_b, lru_conv_w, lru_conv_b, lru_wa, lru_ba,
           lru_wx, lru_bx, lru_lam, g_a, g_b, w_out, norm_mem, w_kv, norm_xa, w_q, w_o, norm_ffn, w_up,
           ffn_conv_w, ffn_conv_b, w_down, norm_final):
    global _NC
    f = lambda a: np.ascontiguousarray(np.asarray(a, np.float32))
    x_prompt, x_sample, mem_prompt = f(x_prompt), f(x_sample), f(mem_prompt)
    if _NC is None:
        _NC = build_program()
    nc = _NC
    shared = {
        "w_in": f(w_in[0]), "w_out": f(w_out[0]), "w_kv": f(w_kv[0]), "w_q": f(w_q[0]), "w_o": f(w_o[0]),
        "w_up": f(w_up[0]), "w_down": f(w_down[0]),
        "wmT": f(np.transpose(np.asarray(gmlp_w[0]), (2, 0, 1)).reshape(128, 1024)),
        "lwa": f(np.transpose(np.asarray(lru_wa[0]), (1, 0, 2)).reshape(128, 1024)),
        "lwx": f(np.transpose(np.asarray(lru_wx[0]), (1, 0, 2)).reshape(128, 1024)),
    }
    cstb = np.zeros((128, 2560), np.float32)
    cstb[:, 0:1024] = np.asarray(g_v[0])[None, :]
    cstb[:, 1024:2048] = np.asarray(gmlp_b[0]).reshape(1, 1024)
    cstb[:, 2048:2560] = np.tile(np.asarray(gmlp_b[0])[:, None, 0:16], (1, 4, 1)).reshape(1, 512)
    shared["cstb"] = cstb
    cbase = np.zeros((128, NCST), np.float32)
    cbase[:, C_GMIX:C_GMIX + 16] = _fm(norm_mix[0], 16)
    cbase[:, C_GXA:C_GXA + 16] = _fm(norm_xa[0], 16)
    cbase[:, C_GFFN:C_GFFN + 16] = _fm(norm_ffn[0], 16)
    cbase[:, C_GFIN:C_GFIN + 16] = _fm(norm_final, 16)
    cbase[:, C_GMEM:C_GMEM + 16] = _fm(norm_mem[0], 16)
    cbase[:, C_GA:C_GA + 8] = _fm(g_a[0], 8)
    cbase[:, C_GB:C_GB + 8] = _fm(g_b[0], 8)
    cw = np.asarray(lru_conv_w[0], np.float32)
    cbase[:, C_CW:C_CW + 32] = cw.reshape(4, 8, 128).transpose(2, 1, 0).reshape(128, 32)
    cbase[:, C_CB:C_CB + 8] = _fm(lru_conv_b[0], 8)
    cbase[:, C_BA:C_BA + 8] = _fm(lru_ba[0], 8)
    cbase[:, C_BX:C_BX + 8] = _fm(lru_bx[0], 8)
    cbase[:, C_LAM:C_LAM + 8] = _fm(lru_lam[0], 8)
    fw = np.asarray(ffn_conv_w[0], np.float32)
    cbase[:, C_FCW:C_FCW + 288] = fw.reshape(3, 96, 128).transpose(2, 1, 0).reshape(128, 288)
    cbase[:, C_FCB:C_FCB + 96] = _fm(ffn_conv_b[0], 96)
    in_maps = []
    for c in range(8):
        b, s = c // 4, c % 4
        start = 1024 * s
        xin = np.zeros((NIN, D), np.float32)
        lo = start - (NPRE + NHALO)
        src_lo = max(lo, 0)
        xin[src_lo - lo:NPRE + NHALO + NP_] = x_prompt[b, src_lo:start + NP_]
        xin[NPRE + NHALO + NP_:] = x_sample[4 * c:4 * c + 4].reshape(NS, D)
        cst = cbase.copy()
        for ti, (c0, c1) in enumerate(PRE_TILES):
            cst[:, C_MASK + ti] = 1.0 if (lo + c0) >= 0 else 0.0
        cst[:, C_MASK + 6] = 1.0 if s >= 1 else 0.0
        hs = np.asarray(state_lru_h[0, 4 * c:4 * c + 4], np.float32)
        cst[:, C_HST:C_HST + 32] = hs.reshape(4, 8, 128).transpose(2, 1, 0).reshape(128, 32)
        sxr = np.asarray(state_lru_conv[0, 4 * c:4 * c + 4], np.float32)
        sfc = np.asarray(state_ffn_conv[0, 4 * c:4 * c + 4], np.float32)
        m = {
            "xin": np.ascontiguousarray(xin.T),
            "memT": np.ascontiguousarray(mem_prompt[b].T),
            "cst": cst,
            "skT": np.ascontiguousarray(np.asarray(cache_mem_k[0, 4 * c:4 * c + 4], np.float32).reshape(4, NMEM, D).transpose(0, 2, 1)),
            "sv": np.ascontiguousarray(np.asarray(cache_mem_v[0, 4 * c:4 * c + 4], np.float32).reshape(4, NMEM, D)),
            "sxr": np.ascontiguousarray(sxr.reshape(4, 3, 8, 128).transpose(3, 2, 0, 1).reshape(128, 96)),
            "sfc": np.ascontiguousarray(sfc.reshape(4, 2, 96, 128).transpose(3, 2, 0, 1).reshape(128, 768)),
        }
        m.update(shared)
        in_maps.append(m)
    res = run_bass_kernel_spmd(nc, in_maps, core_ids=list(range(8)))
    R = res.results
    y_prompt = np.zeros((2, 4096, D), np.float32)
    y_sample = np.zeros((32, 16, D), np.float32)
    mk = np.zeros((1, 2, NMEM, 4, 512), np.float32)
    mv = np.zeros((1, 2, NMEM, 4, 512), np.float32)
    lh_p = np.zeros((1, 2, DB), np.float32)
    lc_p = np.zeros((1, 2, 3, DB), np.float32)
    fc_p = np.zeros((1, 2, 2, 2 * DFF), np.float32)
    lh_s = np.zeros((1, 32, DB), np.float32)
    lc_s = np.zeros((1, 32, 3, DB), np.float32)
    fc_s = np.zeros((1, 32, 2, 2 * DFF), np.float32)
    gv_s = np.zeros((1, 32, 16, DA), np.float32)
    for c in range(8):
        b, s = c // 4, c % 4
        r = R[c]
        yT = np.asarray(r["yT"])
        y_prompt[b, 1024 * s:1024 * s + 1024] = yT[:, 0:1024].T
        y_sample[4 * c:4 * c + 4] = yT[:, 1024:1088].T.reshape(4, 16, D)
        lruh = np.asarray(r["lruh"]).reshape(128, 8, 5)
        lruc = np.asarray(r["lruc"]).reshape(128, 8, 5, 3)
        ffnc = np.asarray(r["ffnc"]).reshape(128, 96, 5, 2)
        if s == 0:
            mk[0, b] = np.asarray(r["mkT"]).T.reshape(NMEM, 4, 512)
            mv[0, b] = np.asarray(r["mv"]).reshape(NMEM, 4, 512)
        if s == 3:
            lh_p[0, b] = lruh[:, :, 0].T.reshape(DB)
            lc_p[0, b] = lruc[:, :, 0, :].transpose(2, 1, 0).reshape(3, DB)
            fc_p[0, b] = ffnc[:, :, 0, :].transpose(2, 1, 0).reshape(2, 2 * DFF)
        lh_s[0, 4 * c:4 * c + 4] = lruh[:, :, 1:5].transpose(2, 1, 0).reshape(4, DB)
        lc_s[0, 4 * c:4 * c + 4] = lruc[:, :, 1:5, :].transpose(2, 3, 1, 0).reshape(4, 3, DB)
        fc_s[0, 4 * c:4 * c + 4] = ffnc[:, :, 1:5, :].transpose(2, 3, 1, 0).reshape(4, 2, 2 * DFF)
        gv_s[0, 4 * c:4 * c + 4] = np.asarray(r["gv"]).reshape(4, 16, DA)
    return (y_prompt, y_sample, mk, mv, lh_p, lc_p, fc_p, lh_s, lc_s, fc_s, gv_s)
```

```python
import numpy as np
import concourse.bass as bass
import concourse.mybir as mybir
from concourse.bass_utils import run_bass_kernel_spmd

F32 = mybir.dt.float32
BF16 = mybir.dt.bfloat16
AF = mybir.ActivationFunctionType
ALU = mybir.AluOpType

D = 2048
DA = 1024
DB = 1024
DFF = 6144
NMEM = 256
EPS = 1e-6
NPRE = 2944
NHALO = 128
NP_ = 1024
NS = 64
NIN = NPRE + NHALO + NP_ + NS
PRE_TILES = [(0, 512), (512, 1024), (1024, 1536), (1536, 2048), (2048, 2560), (2560, 2944)]

C_GMIX, C_GXA, C_GFFN, C_GFIN, C_GMEM = 0, 16, 32, 48, 64
C_GA, C_GB = 80, 88
C_CW = 96
C_CB = 128
C_BA = 136
C_BX = 144
C_LAM = 152
C_FCW = 160
C_FCB = 448
C_MASK = 544
C_HST = 552
NCST = 584


class V:
    __slots__ = ("ap", "sp", "lo", "hi")

    def __init__(self, ap, sp, lo, hi):
        self.ap, self.sp, self.lo, self.hi = ap, sp, lo, hi


class Buf:
    def __init__(self, ap, sp, base, shape, esz):
        self.ap, self.sp, self.base, self.shape, self.esz = ap, sp, base, shape, esz
        self.rowlen = 1
        for s in shape[2:]:
            self.rowlen *= s

    def all(self):
        n = 1
        for s in self.shape[1:]:
            n *= s
        return V(self.ap, self.sp, self.base, self.base + n * self.esz)

    def __call__(self, k=None, a=None, b=None, k1=None, p0=0, p1=None):
        sh = self.shape
        p1 = sh[0] if p1 is None else p1
        if len(sh) == 2:
            a = 0 if a is None else a
            b = sh[1] if b is None else b
            return V(self.ap[p0:p1, a:b], self.sp, self.base + a * self.esz, self.base + b * self.esz)
        n = sh[2]
        a = 0 if a is None else a
        b = n if b is None else b
        if k1 is None:
            lo = self.base + (k * n + a) * self.esz
            return V(self.ap[p0:p1, k, a:b], self.sp, lo, lo + (b - a) * self.esz)
        lo = self.base + (k * n + a) * self.esz
        hi = self.base + ((k1 - 1) * n + b) * self.esz
        return V(self.ap[p0:p1, k:k1, a:b], self.sp, lo, hi)

    def raw(self, ap):
        v = self.all()
        return V(ap, v.sp, v.lo, v.hi)


class Op:
    __slots__ = ("eng", "fn", "deps", "dma_sem", "dma_val", "signal", "sigval", "idx")


class Sched:
    def __init__(self):
        self.ops = []
        self.recs = {}
        self.dma_counts = {}
        self.psum_rr = 0

    def _collect(self, op, v, write):
        if v.sp == "ps":
            bank = v.lo // 2048
            for d in self.recs.get(("psb", bank), []):
                if d.eng != op.eng:
                    op.deps.add(d)
            return
        for r in self.recs.get(v.sp, []):
            if r[0] < v.hi and v.lo < r[1]:
                if r[2] is not None:
                    op.deps.add(r[2])
                if write:
                    op.deps.update(r[3])

    def _commit(self, op, v, write):
        if v.sp == "ps":
            bank = v.lo // 2048
            recs = self.recs.setdefault(("psb", bank), [])
            self.recs[("psb", bank)] = [d for d in recs if d.eng != op.eng] + [op]
            return
        recs = self.recs.setdefault(v.sp, [])
        if write:
            keep = []
            for r in recs:
                if r[0] < v.hi and v.lo < r[1]:
                    if r[0] < v.lo:
                        keep.append([r[0], v.lo, r[2], list(r[3])])
                    if r[1] > v.hi:
                        keep.append([v.hi, r[1], r[2], list(r[3])])
                    continue
                keep.append(r)
            keep.append([v.lo, v.hi, op, []])
            self.recs[v.sp] = keep
        else:
            for r in recs:
                if r[0] < v.hi and v.lo < r[1]:
                    rd = r[3]
                    if op in rd:
                        continue
                    if op.dma_sem is None:
                        for i, x in enumerate(rd):
                            if x.dma_sem is None and x.eng == op.eng:
                                rd[i] = op
                                break
                        else:
                            rd.append(op)
                    else:
                        rd.append(op)

    def op(self, eng, fn, R=(), W=(), dma_key=None):
        o = Op()
        o.eng, o.fn, o.deps = eng, fn, set()
        o.dma_sem, o.dma_val, o.signal, o.sigval = dma_key, 0, False, 0
        o.idx = len(self.ops)
        for v in R:
            self._collect(o, v, False)
        for v in W:
            self._collect(o, v, True)
        for v in R:
            self._commit(o, v, False)
        for v in W:
            self._commit(o, v, True)
        o.deps.discard(o)
        latest = {}
        red = set()
        for d in o.deps:
            if d.dma_sem is not None:
                red.add(d)
            elif d.eng not in latest or latest[d.eng].idx < d.idx:
                latest[d.eng] = d
        red.update(latest.values())
        o.deps = red
        if dma_key is not None:
            c = self.dma_counts.get(dma_key, 0) + 16
            self.dma_counts[dma_key] = c
            o.dma_val = c
        self.ops.append(o)
        return o

    def finalize(self):
        for o in self.ops:
            for d in o.deps:
                if d.dma_sem is None:
                    if d.eng == o.eng and d.eng == "pe":
                        continue
                    d.signal = True
        cnt = {}
        for o in self.ops:
            if o.dma_sem is None and o.signal:
                cnt[o.eng] = cnt.get(o.eng, 0) + 1
                o.sigval = cnt[o.eng]


def split_cols(c0, c1):
    n = c1 - c0
    if n <= 512:
        return [(c0, c1)]
    h = (n + 1) // 2
    return [(c0, c0 + h), (c0 + h, c1)]


class _Stop(Exception):
    pass


MARKS = []


def build_program(stop=99):
    nc = bass.Bass("TRN2", target_bir_lowering=False)
    S = Sched()

    def chk(n):
        MARKS.append((n, sum(1 for o in S.ops if o.eng == "pe")))
        if n > stop:
            raise _Stop()

    def din(name, shape):
        return nc.dram_tensor(name, list(shape), F32, kind="ExternalInput").ap()

    def dout(name, shape):
        return nc.dram_tensor(name, list(shape), F32, kind="ExternalOutput").ap()

    xin = din("xin", [D, NIN])
    memT = din("memT", [D, NMEM])
    cst_d = din("cst", [128, NCST])
    cstb_d = din("cstb", [128, 1024 + 1024 + 512])
    wmT_d = din("wmT", [128, 8 * 128])
    lwa_d = din("lwa", [128, 8 * 128])
    lwx_d = din("lwx", [128, 8 * 128])
    skT_d = din("skT", [4, D, NMEM])
    sv_d = din("sv", [4, NMEM, D])
    sxr_d = din("sxr", [128, 8 * 4 * 3])
    sfc_d = din("sfc", [128, 96 * 4 * 2])
    w_in = din("w_in", [D, 4096])
    w_out = din("w_out", [D, D])
    w_kv = din("w_kv", [D, 4096])
    w_q = din("w_q", [D, D])
    w_o = din("w_o", [D, D])
    w_up = din("w_up", [D, 2 * DFF])
    w_down = din("w_down", [DFF, D])

    yT_o = dout("yT", [D, NP_ + NS])
    mkT_o = dout("mkT", [D, NMEM])
    mv_o = dout("mv", [NMEM, D])
    lruh_o = dout("lruh", [128, 8 * 5])
    lruc_o = dout("lruc", [128, 8 * 5 * 3])
    ffnc_o = dout("ffnc", [128, 96 * 5 * 2])
    gv_o = dout("gv", [NS, DA])

    kv_scr = nc.dram_tensor("kv_scr", [128, 16 * 256 + 2 * 2048], BF16).ap()

    ARENA_F32 = 53000
    ctx_arena = nc.sbuf_tensor("arena", [128, ARENA_F32], F32)
    arena = ctx_arena.__enter__()
    psum_ctx = [nc.psum_tensor("ps%d" % i, [128, 512], F32) for i in range(8)]
    psum_t = [c.__enter__() for c in psum_ctx]
    PS = [Buf(psum_t[i][:], "ps", i * 2048, [128, 512], 4) for i in range(8)]

    top = [0]

    def alloc(shape, dt):
        esz = 4 if dt == F32 else 2
        n = 1
        for s in shape[1:]:
            n *= s
        nbytes = (n * esz + 63) // 64 * 64
        off = top[0]
        top[0] += nbytes
        assert top[0] <= ARENA_F32 * 4, ("arena overflow", top[0])
        ap = arena[:, off // 4:(off + nbytes) // 4]
        if dt == BF16:
            ap = ap.bitcast(BF16)
        ap = ap[:, 0:n]
        if len(shape) == 3:
            ap = ap.rearrange("p (k n) -> p k n", k=shape[1])
        elif len(shape) == 4:
            ap = ap.rearrange("p (k b n) -> p k b n", k=shape[1], b=shape[2])
        if shape[0] < 128:
            ap = ap[0:shape[0]]
        return Buf(ap, "sb", off, list(shape), esz)

    def psum():
        b = PS[S.psum_rr % 8]
        S.psum_rr += 1
        return b

    def dma(q, out_ap, in_ap, key, R=(), W=()):
        S.op(q, lambda e: e.dma_start(out=out_ap, in_=in_ap), R, W, dma_key=key + "_" + q)

    def mm(out, lhsT, rhs, start, stop):
        bank = out.lo // 2048
        wv = V(out.ap, "ps", bank * 2048, bank * 2048 + 2048)
        S.op("pe", lambda e: e.matmul(out.ap, lhsT.ap, rhs.ap, start=start, stop=stop), [lhsT, rhs], [wv])

    def act(out, in_, func, bias=None, scale=None, accum=None, extraR=()):
        kw = {}
        R = [in_] + list(extraR)
        W = [out]
        if bias is not None:
            kw["bias"] = bias.ap if isinstance(bias, V) else bias
            if isinstance(bias, V):
                R.append(bias)
        if scale is not None:
            kw["scale"] = scale.ap if isinstance(scale, V) else scale
            if isinstance(scale, V):
                R.append(scale)
        if accum is not None:
            kw["accum_out"] = accum.ap
            W.append(accum)
        S.op("act", lambda e: e.activation(out=out.ap, in_=in_.ap, func=func, **kw), R, W)

    def ts(out, in0, s1, s2, op0, op1=None, eng="dve"):
        R = [in0]
        a1 = s1.ap if isinstance(s1, V) else s1
        a2 = s2.ap if isinstance(s2, V) else s2
        if isinstance(s1, V):
            R.append(s1)
        if isinstance(s2, V):
            R.append(s2)
        if op1 is None:
            S.op(eng, lambda e: e.tensor_scalar(out=out.ap, in0=in0.ap, scalar1=a1, scalar2=None, op0=op0), R, [out])
        else:
            S.op(eng, lambda e: e.tensor_scalar(out=out.ap, in0=in0.ap, scalar1=a1, scalar2=a2, op0=op0, op1=op1), R, [out])

    def stt(out, in0, sc, in1, op0, op1, eng="dve"):
        R = [in0, in1]
        a = sc.ap if isinstance(sc, V) else sc
        if isinstance(sc, V):
            R.append(sc)
        S.op(eng, lambda e: e.scalar_tensor_tensor(out=out.ap, in0=in0.ap, scalar=a, in1=in1.ap, op0=op0, op1=op1), R, [out])

    def tt(out, in0, in1, op, eng="dve"):
        S.op(eng, lambda e: e.tensor_tensor(out=out.ap, in0=in0.ap, in1=in1.ap, op=op), [in0, in1], [out])

    def cp(out, in_, eng="dve"):
        if eng == "act":
            S.op(eng, lambda e: e.activation(out=out.ap, in_=in_.ap, func=AF.Copy), [in_], [out])
        else:
            S.op(eng, lambda e: e.tensor_copy(out=out.ap, in_=in_.ap), [in_], [out])

    def recip(out, in_):
        S.op("dve", lambda e: e.reciprocal(out=out.ap, in_=in_.ap), [in_], [out])

    def scan(out, d0, d1, init):
        R = [d0, d1]
        a = init.ap if isinstance(init, V) else init
        if isinstance(init, V):
            R.append(init)
        S.op("dve", lambda e: e.tensor_tensor_scan(out=out.ap, data0=d0.ap, data1=d1.ap, initial=a, op0=ALU.mult, op1=ALU.add), R, [out])

    def memset(v, val, eng="dve"):
        S.op(eng, lambda e: e.memset(v.ap, val), [], [v])

    cst = alloc([128, NCST], F32)
    ones_bf = alloc([128, 128], BF16)
    wmT = alloc([128, 8, 128], BF16)
    wsT = alloc([64, 8, 64], BF16)
    lwa = alloc([128, 8, 128], BF16)
    lwx = alloc([128, 8, 128], BF16)
    drv = alloc([128, 48], F32)
    hstate = alloc([128, 8], F32)
    xr_hist = alloc([128, 8, 3], F32)
    up_hist = alloc([128, 96, 2], F32)
    xrs = alloc([128, 8, 4, 19], F32)
    fcst = alloc([128, 96, 4, 2], F32)
    fcs = alloc([128, 1, 4, 18], F32)
    o_lruh = alloc([128, 8, 5], F32)
    o_lruc = alloc([128, 8, 5, 3], F32)
    o_ffnc = alloc([128, 96, 5, 2], F32)
    mcb = alloc([128, 7, 8], F32)
    wb = [alloc([128, 16, 512], BF16), alloc([128, 16, 512], BF16)]
    wslot = [0]
    wslots = list(wb)

    def set_extra_slots(n):
        del wslots[2:]
        for _ in range(n):
            wslots.append(alloc([128, 16, 512], BF16))
    P_MARK = top[0]

    def cc(col, n=1):
        return cst(a=col, b=col + n)

    def dv(col, n=1):
        return drv(a=col, b=col + n)

    HC, C1, HBA, HBX, DEPS, DQ, DH = 0, 8, 16, 24, 32, 33, 34

    dma("sp", cst.ap, cst_d, "cst", W=[cst.all()])
    dma("sp", xrs.ap[:, :, :, 0:3], sxr_d.rearrange("p (k b n) -> p k b n", k=8, b=4), "cxrs", W=[xrs.all()])
    dma("sp", fcst.ap, sfc_d.rearrange("p (k b n) -> p k b n", k=96, b=4), "cfcs", W=[fcst.all()])
    dma("pool", wmT.ap, wmT_d.rearrange("p (g i) -> p g i", g=8), "cw", W=[wmT.all()])
    dma("pool", lwa.ap, lwa_d.rearrange("p (g i) -> p g i", g=8), "cwa", W=[lwa.all()])
    dma("pool", lwx.ap, lwx_d.rearrange("p (g i) -> p g i", g=8), "cwx", W=[lwx.all()])
    memset(wsT.all(), 0.0)
    wm3 = wmT_d.rearrange("p (g i) -> p g i", g=8)
    for b in range(4):
        dma("pool", wsT.ap[16 * b:16 * b + 16, :, 16 * b:16 * b + 16], wm3[0:16, :, 0:16], "cw2",
            W=[wsT.all()], R=[])
    memset(wmT.raw(wmT.ap[64:128, :, 0:64]), 0.0)
    memset(ones_bf.all(), 1.0)
    memset(dv(DEPS), EPS)
    memset(dv(DQ), 0.25)
    memset(hstate.all(), 0.0)
    memset(xr_hist.all(), 0.0)
    memset(up_hist.all(), 0.0)
    act(dv(HC, 8), cc(C_LAM, 8), AF.Exp, scale=-1.0)
    act(dv(C1, 8), dv(HC, 8), AF.Ln, bias=1.0)
    ts(dv(HC, 8), dv(C1, 8), -4.0, None, ALU.mult)
    ts(dv(C1, 8), dv(C1, 8), -8.0, None, ALU.mult)
    ts(dv(HBA, 8), cc(C_BA, 8), 0.5, None, ALU.mult)
    ts(dv(HBX, 8), cc(C_BX, 8), 0.5, None, ALU.mult)
    for m_ in range(7):
        ts(mcb(m_), cc(C_CB, 8), cc(C_MASK + m_), None, ALU.mult)

    def load_w(src_ap, kch=16, ncol=512):
        si_ = wslot[0] % len(wslots)
        slot = wslots[si_]
        wslot[0] += 1
        dst = slot.ap if (kch == 16 and ncol == 512) else \
            slot.ap.rearrange("p k n -> p (k n)")[:, 0:kch * ncol].rearrange("p (k n) -> p k n", k=kch)
        dma("pool", dst, src_ap.rearrange("(kc p) n -> p kc n", p=128), "w%d" % si_,
            W=[slot.all()])
        return Buf(dst, "sb", slot.base, [128, kch, ncol], 2)

    def rmsnorm_fm(xb, gcol, outb, c0, c1, nk, dim, tmp_sq, rstd, sqb=None):
        sqb = outb if sqb is None else sqb
        for ti, (a, b) in enumerate(split_cols(c0, c1)):
            n = b - a
            ps = psum()
            act(sqb(0, a, b, k1=nk), xb(0, a, b, k1=nk), AF.Square)
            for k in range(nk):
                mm(ps(a=0, b=n), ones_bf.all(), sqb(k, a, b), k == 0, k == nk - 1)
            act(rstd(ti, 0, n), ps(a=0, b=n), AF.Sqrt, bias=dv(DEPS), scale=1.0 / dim)
            recip(rstd(ti, 0, n), rstd(ti, 0, n))
            for k in range(nk):
                stt(outb(k, a, b), xb(k, a, b), cc(gcol + k), rstd(ti, 0, n), ALU.mult, ALU.mult)

    def lru_heads(heads, hsrc, wsrc_fn, ncols, prompt_n, mask_col, mask_n, L, sample, rstd=None):
        tiles = split_cols(0, ncols)
        xb, xc, xcbf = L["xrbuf"], L["xc"], L["xcbf"]
        tha, thx, A_, A2 = L["tha"], L["thx"], L["a"], L["a2"]
        hs = L["hseq"]
        n = prompt_n
        H = list(enumerate(heads))
        for kl, k in H:
            wbuf, wc0 = wsrc_fn(kl)
            cp(xb(kl, 0, 3), xr_hist(k), eng="act")
            for (a, b) in tiles:
                ps = psum()
                for kc in range(16):
                    mm(ps(a=0, b=b - a), wbuf(kc, wc0, wc0 + 128), hsrc(kc, a, b), kc == 0, kc == 15)
                pa, pb = a, min(b, prompt_n)
                if pb > pa:
                    if rstd is None:
                        cp(xb(kl, 3 + pa, 3 + pb), ps(a=0, b=pb - pa), eng="act")
                    else:
                        tt(xb(kl, 3 + pa, 3 + pb), ps(a=0, b=pb - pa), rstd(a=pa, b=pb), ALU.mult)
                if sample and b > prompt_n:
                    sa = max(a, prompt_n)
                    assert sa == prompt_n and b == prompt_n + 64
                    src = ps.ap[:, sa - a:b - a].rearrange("p (b t) -> p b t", b=4)
                    S.op("act", lambda e, o=xrs.ap[:, k, :, 3:19], i=src: e.activation(out=o, in_=i, func=AF.Copy),
                         [ps.all()], [xrs.all()])
        yield
        for kl, k in H:
            if mask_n > 0:
                ts(xc(kl, 0, mask_n), xb(kl, 0, mask_n), cc(C_CW + 4 * k), mcb(mask_col, k, k + 1), ALU.mult, ALU.add)
            if n > mask_n:
                ts(xc(kl, mask_n, n), xb(kl, mask_n, n), cc(C_CW + 4 * k), cc(C_CB + k), ALU.mult, ALU.add)
        for j in range(1, 4):
            for kl, k in H:
                stt(xc(kl, 0, n), xb(kl, j, j + n), cc(C_CW + 4 * k + j), xc(kl, 0, n), ALU.mult, ALU.add)
        for kl, k in H:
            cp(xr_hist(k), xb(kl, n, n + 3), eng="act")
        if sample:
            for kl, k in H:
                xo = xc.ap[:, kl, n:n + 64].rearrange("p (b t) -> p b t", b=4)
                xov = xc(kl, n, n + 64)
                S.op("dve", lambda e, o=xo, i=xrs.ap[:, k, :, 0:16], s1=cc(C_CW + 4 * k).ap, s2=cc(C_CB + k).ap:
                     e.tensor_scalar(out=o, in0=i, scalar1=s1, scalar2=s2, op0=ALU.mult, op1=ALU.add),
                     [xrs.all(), cst.all()], [xov])
                for j in range(1, 4):
                    S.op("dve", lambda e, o=xo, i=xrs.ap[:, k, :, j:j + 16], s1=cc(C_CW + 4 * k + j).ap:
                         e.scalar_tensor_tensor(out=o, in0=i, scalar=s1, in1=o, op0=ALU.mult, op1=ALU.add),
                         [xrs.all(), cst.all(), xov], [xov])
                S.op("act", lambda e, o=o_lruc.ap[:, k, 1:5, :], i=xrs.ap[:, k, :, 16:19]:
                     e.activation(out=o, in_=i, func=AF.Copy), [xrs.all()], [o_lruc.all()])
        for kl, k in H:
            cp(xcbf(kl, 0, ncols), xc(kl, 0, ncols), eng="act")
        yield
        for (a, b) in tiles:
            nn = b - a
            pp = {}
            for kl, k in H:
                psa, psx = psum(), psum()
                mm(psa(a=0, b=nn), lwa(k), xcbf(kl, a, b), True, True)
                mm(psx(a=0, b=nn), lwx(k), xcbf(kl, a, b), True, True)
                pp[kl] = (psa, psx)
                if kl % 2 == 1 or kl == len(heads) - 1:
                    for kk in ([kl - 1, kl] if kl % 2 == 1 else [kl]):
                        pa_, px_ = pp[kk]
                        kg = heads[kk]
                        act(tha(kk, 0, nn), pa_(a=0, b=nn), AF.Tanh, bias=dv(HBA + kg), scale=0.5)
                        act(thx(kk, 0, nn), px_(a=0, b=nn), AF.Tanh, bias=dv(HBX + kg), scale=0.5)
            for kl, k in H:
                act(A_(kl, a, b), tha(kl, 0, nn), AF.Exp, bias=dv(HC + k), scale=dv(HC + k))
                act(A2(kl, a, b), tha(kl, 0, nn), AF.Exp, bias=dv(C1 + k), scale=dv(C1 + k))
            for kl, k in H:
                stt(xc(kl, a, b), thx(kl, 0, nn), 1.0, xc(kl, a, b), ALU.add, ALU.mult)
        yield
        for kl, k in H:
            act(A2(kl, 0, ncols), A2(kl, 0, ncols), AF.Sqrt, bias=dv(DQ), scale=-0.25)
        yield
        for kl, k in H:
            tt(xc(kl, 0, ncols), xc(kl, 0, ncols), A2(kl, 0, ncols), ALU.mult)
        for kl, k in H:
            hk = L["hidx"](k, kl)
            scan(hs(hk, 0, prompt_n), A_(kl, 0, prompt_n), xc(kl, 0, prompt_n), hstate(a=k, b=k + 1))
        for kl, k in H:
            hk = L["hidx"](k, kl)
            cp(hstate(a=k, b=k + 1), hs(hk, prompt_n - 1, prompt_n), eng="act")
        if sample:
            for kl, k in H:
                hk = L["hidx"](k, kl)
                for b in range(4):
                    c0 = prompt_n + 16 * b
                    scan(hs(hk, c0, c0 + 16), A_(kl, c0, c0 + 16), xc(kl, c0, c0 + 16),
                         cc(C_HST + 4 * k + b))
                    cp(o_lruh.raw(o_lruh.ap[:, k, 1 + b:2 + b]), hs(hk, c0 + 15, c0 + 16), eng="act")

    def run_gens(gens):
        gens = list(gens)
        while gens:
            nxt = []
            for g in gens:
                try:
                    next(g)
                    nxt.append(g)
                except StopIteration:
                    pass
            gens = nxt

    def phases():
        chk(1)
        phase_kv()
        chk(2)
        phase_pre()
        chk(3)
        supertile(0)
        chk(10)
        supertile(1)

    KV_D = V(kv_scr, "dram_kv", 0, 1)

    def phase_kv():
      top[0] = P_MARK
      if True:
        mem_f = alloc([128, 16, NMEM], F32)
        hmem = alloc([128, 16, NMEM], BF16)
        sq_t = None
        rs_t = alloc([128, 2, 512], F32)
        kt_bf = alloc([128, 16, NMEM], BF16)
        v_bf = alloc([128, 2, D], BF16)
        stg = [alloc([128, 512], F32), alloc([128, 512], F32)]
        import os
        KVL = int(os.environ.get("KV_LEVEL", "9"))
        dma("sp", mem_f.ap, memT.rearrange("(kc p) m -> p kc m", p=128), "memf", W=[mem_f.all()])
        if KVL < 1:
            return
        rmsnorm_fm(mem_f, C_GMEM, hmem, 0, NMEM, 16, D, sq_t, rs_t)
        if KVL < 2:
            return
        si = 0
        for t in range(4 if KVL >= 3 else 1):
            w = load_w(w_kv[:, 512 * t:512 * t + 512])
            for m in range(4):
                nch = 4 * t + m
                ps = psum()
                for kc in range(16):
                    mm(ps(a=0, b=NMEM), w(kc, m * 128, m * 128 + 128), hmem(kc), kc == 0, kc == 15)
                sg = stg[si % 2]
                si += 1
                cp(sg(a=0, b=NMEM), ps(a=0, b=NMEM), eng="act")
                cp(kt_bf(nch), ps(a=0, b=NMEM))
                dma("sp", mkT_o[nch * 128:(nch + 1) * 128, :], sg.ap[:, 0:NMEM], "stg%d" % ((si - 1) % 2), R=[sg.all()])
        if KVL < 4:
            return
        for t in range(4):
            w = load_w(w_kv[:, 2048 + 512 * t:2048 + 512 * t + 512])
            for mc in range(2):
                ps = psum()
                for kc in range(16):
                    mm(ps.all(), hmem(kc, mc * 128, mc * 128 + 128), w(kc), kc == 0, kc == 15)
                sg = stg[si % 2]
                si += 1
                cp(sg.all(), ps.all(), eng="act")
                cp(v_bf(mc, 512 * t, 512 * t + 512), ps.all())
                dma("sp", mv_o[mc * 128:(mc + 1) * 128, 512 * t:512 * t + 512], sg.ap, "stg%d" % ((si - 1) % 2), R=[sg.all()])

        import os
        if os.environ.get("KV_NOSCR"):
            return
        dma("sp", kv_scr[:, 0:4096], kt_bf.ap.rearrange("p k n -> p (k n)"), "kvs", R=[kt_bf.all()], W=[KV_D])
        dma("sp", kv_scr[:, 4096:8192], v_bf.ap.rearrange("p k n -> p (k n)"), "kvs", R=[v_bf.all()], W=[KV_D])

    def phase_pre():
      set_extra_slots(0)
      if True:
        top[0] = P_MARK
        wxr = Buf(arena[:, wb[0].base // 4:(wb[0].base + 32768) // 4].bitcast(BF16).rearrange("p (k n) -> p k n", k=16),
                  "sb", wb[0].base, [128, 16, 1024], 2)
        assert wb[1].base == wb[0].base + 16384
        xt = [alloc([128, 16, 512], BF16), alloc([128, 16, 512], BF16)]
        h1p = alloc([128, 16, 512], BF16)
        sq_t = None
        rs_t = alloc([128, 2, 512], F32)
        def mkL():
            d = {
                "xrbuf": alloc([128, 4, 3 + 512], F32), "xc": alloc([128, 4, 512], F32), "xcbf": alloc([128, 4, 512], BF16),
                "tha": alloc([128, 4, 512], F32), "thx": alloc([128, 4, 512], F32),
                "a": alloc([128, 4, 512], F32),
                "hidx": (lambda k, kl: kl),
            }
            d["hseq"] = d["xc"]
            d["a2"] = Buf(d["xrbuf"].ap[:, :, 0:512], "sb", d["xrbuf"].base, [128, 4, 515], 4)
            return d
        Ls = [mkL(), mkL()]
        for hlf in range(2):
            dma("pool", wxr.ap[:, :, 512 * hlf:512 * hlf + 512],
                w_in[:, 2048 + 512 * hlf:2048 + 512 * hlf + 512].rearrange("(kc p) n -> p kc n", p=128), "wxr",
                W=[wxr.all()])
        for kc in range(16):
            ts(wxr(kc), wxr(kc), cc(C_GMIX + kc), None, ALU.mult)
        def step(g):
            if g is None:
                return
            try:
                next(g)
            except StopIteration:
                pass

        prev = None
        for ti, (c0, c1) in enumerate(PRE_TILES):
            n = c1 - c0
            xb = xt[ti % 2]
            dma("pool", xb.ap[:, :, 0:n], xin[:, c0:c1].rearrange("(kc p) n -> p kc n", p=128), "x%d" % (ti % 2),
                W=[xb.all()])
            ps = psum()
            act(h1p(0, 0, n, k1=16), xb(0, 0, n, k1=16), AF.Square)
            for k in range(16):
                mm(ps(a=0, b=n), ones_bf.all(), h1p(k, 0, n), k == 0, k == 15)
            rstd = Buf(rs_t.ap[:, ti % 2, :], "sb", rs_t.base + (ti % 2) * 512 * 4, [128, 512], 4)
            act(rstd(a=0, b=n), ps(a=0, b=n), AF.Sqrt, bias=dv(DEPS), scale=1.0 / D)
            recip(rstd(a=0, b=n), rstd(a=0, b=n))
            G = [lru_heads([4 * hg + i for i in range(4)], xb, lambda kl, hg=hg: (wxr, 512 * hg + 128 * kl),
                           n, n, ti, n, Ls[hg], False, rstd=rstd) for hg in range(2)]
            step(G[0])
            step(prev)
            step(G[0])
            step(prev)
            step(prev)
            step(G[1])
            step(G[0])
            step(G[1])
            step(G[0])
            step(G[0])
            prev = G[1]
        step(prev)
        step(prev)
        step(prev)

    def supertile(st):
        top[0] = P_MARK
        set_extra_slots(0)
        if st == 0:
            xc0, ncm, offp, prompt_n, sample = NPRE, NHALO + 512, NHALO - 2, NHALO + 512, False
            ycol0, ysrc0 = 0, NHALO
        else:
            xc0, ncm, offp, prompt_n, sample = NPRE + NHALO + 512, 512 + NS, 0, 512, True
            ycol0, ysrc0 = 512, 0
        npost = ncm - offp
        xT = alloc([128, 16, 640], F32)
        hb = alloc([128, 16, 640], BF16)
        mixb = alloc([128, 16, 640], BF16)
        sq_t = None
        rs_t = alloc([128, 2, 320], F32)
        T_MARK = top[0]
        dma("sp", xT.ap[:, :, 0:ncm], xin[:, xc0:xc0 + ncm].rearrange("(kc p) n -> p kc n", p=128), "xT",
            W=[xT.all()])
        rmsnorm_fm(xT, C_GMIX, hb, 0, ncm, 16, D, sq_t, rs_t)
        mtiles = split_cols(0, ncm)

        cstb = alloc([128, 2560], F32)
        dma("sp", cstb.ap, cstb_d, "cstb", W=[cstb.all()])
        X = alloc([128, 8, 640], F32)
        vtok = alloc([128, 6, 1024], BF16)
        ssv = alloc([128, 8], F32)
        junk = alloc([128, 1024], BF16)
        utmp2 = [alloc([128, 640], F32), alloc([128, 640], F32)]
        utmp = utmp2[0]
        gvs = alloc([64, 1024], F32)
        nchunk = prompt_n // 128
        chunks = [(c * 128, 128) for c in range(nchunk)] + ([(prompt_n, 64)] if sample else [])
        vg_ap = X.ap.rearrange("p k n -> p (k n)")[:, 0:5 * 1024].rearrange("p (c n) -> p c n", c=5)
        vgS = utmp
        vgs_buf = alloc([128, 1024], F32)
        wv0 = load_w(w_in[:, 1024:1536])
        wv1 = load_w(w_in[:, 1536:2048])
        memset(ssv.all(), 0.0)
        for ci, (t0, tn) in enumerate(chunks):
            for hlf, wv in enumerate((wv0, wv1)):
                ps = psum()
                for kc in range(16):
                    mm(ps(a=0, b=512, p1=tn), hb(kc, t0, t0 + tn), wv(kc), kc == 0, kc == 15)
                if tn == 128:
                    dst = X.raw(vg_ap[:, ci, 512 * hlf:512 * hlf + 512])
                else:
                    dst = vgs_buf(a=512 * hlf, b=512 * hlf + 512, p1=tn)
                act(dst, ps(a=0, b=512, p1=tn), AF.Gelu_apprx_tanh)
            src = X.raw(vg_ap[:, ci, :]) if tn == 128 else vgs_buf(p1=tn)
            act(junk(p1=tn), src, AF.Square, accum=ssv(a=ci, b=ci + 1, p1=tn))
        act(ssv.all(), ssv.all(), AF.Sqrt, bias=dv(DEPS), scale=1.0 / DA)
        recip(ssv.all(), ssv.all())
        gvb = cstb(a=0, b=1024)
        for ci, (t0, tn) in enumerate(chunks):
            src = X.raw(vg_ap[:, ci, :]) if tn == 128 else vgs_buf(p1=tn)
            stt(vtok(ci, p1=tn), src, ssv(a=ci, b=ci + 1, p1=tn), cstb(a=0, b=1024, p1=tn), ALU.mult, ALU.mult)
            if tn == 64:
                stt(gvs.all(), src, ssv(a=ci, b=ci + 1, p1=tn), cstb(a=0, b=1024, p1=tn), ALU.mult, ALU.mult)
                dma("sp", gv_o, gvs.ap, "gvs", R=[gvs.all()])
        for ci, (t0, tn) in enumerate(chunks):
            for gh in range(2):
                ps = psum()
                for gl in range(4):
                    g = 4 * gh + gl
                    if tn == 128:
                        mm(ps(a=128 * gl, b=128 * gl + 128), vtok(ci, 128 * g, 128 * g + 128), wmT(g), True, True)
                    else:
                        mm(ps(a=128 * gl, b=128 * gl + 64), vtok(ci, 128 * g, 128 * g + 128, p1=64),
                           wsT.raw(wsT.ap[:, g, :]), True, True)
                for gl in range(4):
                    g = 4 * gh + gl
                    if tn == 128:
                        tt(X(g, t0, t0 + 128), ps(a=128 * gl, b=128 * gl + 128), cstb(a=1024 + 128 * g, b=1024 + 128 * g + 128), ALU.add)
                    else:
                        tt(X(g, t0, t0 + 64), ps(a=128 * gl, b=128 * gl + 64), cstb(a=2048 + 64 * g, b=2048 + 64 * g + 64), ALU.add)
        for ug in range(2):
            w = load_w(w_in[:, 512 * ug:512 * ug + 512])
            for m in range(4):
                g = 4 * ug + m
                for (a, b) in mtiles:
                    ps = psum()
                    for kc in range(16):
                        mm(ps(a=0, b=b - a), w(kc, 128 * m, 128 * m + 128), hb(kc, a, b), kc == 0, kc == 15)
                    ut = utmp2[g % 2]
                    act(ut(a=a, b=b), ps(a=0, b=b - a), AF.Gelu_apprx_tanh)
                    tt(X(g, a, b), X(g, a, b), ut(a=a, b=b), ALU.mult)
        rmsnorm_fm(X, C_GA, Buf(mixb.ap[:, 0:8, :], "sb", mixb.base, [128, 8, 640], 2), 0, ncm, 8, DA, sq_t, rs_t)

        chk(4 + 10 * st)
        top[0] = T_MARK
        set_extra_slots(0)
        L = {
            "xrbuf": alloc([128, 4, 3 + 640], F32), "xc": alloc([128, 4, 640], F32), "xcbf": alloc([128, 4, 640], BF16),
            "tha": alloc([128, 4, 320], F32), "thx": alloc([128, 4, 320], F32),
            "a": alloc([128, 4, 640], F32), "hseq": alloc([128, 8, 640], F32),
            "hidx": (lambda k, kl: k),
        }
        L["a2"] = Buf(L["xrbuf"].ap[:, :, 0:640], "sb", L["xrbuf"].base, [128, 4, 643], 4)
        _tf = L["tha"].ap.rearrange("p k n -> p (k n)")
        gtmp2 = [Buf(_tf[:, 0:640], "sb", L["tha"].base, [128, 640], 4),
                 Buf(_tf[:, 640:1280], "sb", L["tha"].base + 640 * 4, [128, 640], 4)]
        gg = alloc([128, 2, 640], F32)
        for hg in range(2):
            w = load_w(w_in[:, 2048 + 512 * hg:2048 + 512 * hg + 512])
            wg_ = load_w(w_in[:, 3072 + 512 * hg:3072 + 512 * hg + 512])
            gen = lru_heads([4 * hg + i for i in range(4)], hb, lambda kl, w=w: (w, 128 * kl),
                            ncm, prompt_n, 6, (NHALO if st == 0 else 0), L, sample)
            next(gen)
            for m in range(2):
                for (a, b) in mtiles:
                    ps = psum()
                    for kc in range(16):
                        mm(ps(a=0, b=b - a), wg_(kc, 128 * m, 128 * m + 128), hb(kc, a, b), kc == 0, kc == 15)
                    act(gg(m, a, b), ps(a=0, b=b - a), AF.Gelu_apprx_tanh)
            for _ in gen:
                pass
            for m in range(2):
                k = 4 * hg + m
                tt(L["hseq"](k, 0, ncm), L["hseq"](k, 0, ncm), gg(m, 0, ncm), ALU.mult)
            for m in range(2, 4):
                k = 4 * hg + m
                for (a, b) in mtiles:
                    ps = psum()
                    for kc in range(16):
                        mm(ps(a=0, b=b - a), wg_(kc, 128 * m, 128 * m + 128), hb(kc, a, b), kc == 0, kc == 15)
                    gtmp = gtmp2[m % 2]
                    act(gtmp(a=a, b=b), ps(a=0, b=b - a), AF.Gelu_apprx_tanh)
                    tt(L["hseq"](k, a, b), L["hseq"](k, a, b), gtmp(a=a, b=b), ALU.mult)
        if st == 1:
            for k in range(8):
                cp(o_lruh.raw(o_lruh.ap[:, k, 0:1]), hstate(a=k, b=k + 1), eng="act")
                cp(o_lruc.raw(o_lruc.ap[:, k, 0, :]), xr_hist(k), eng="act")
        rmsnorm_fm(L["hseq"], C_GB, Buf(mixb.ap[:, 8:16, :], "sb", mixb.base + 8 * 640 * 2, [128, 8, 640], 2),
                   0, ncm, 8, DB, sq_t, rs_t)

        chk(5 + 10 * st)
        top[0] = T_MARK
        set_extra_slots(2)
        ptiles = split_cols(offp, ncm)

        def proj_res(wsrc):
            for t in range(4):
                w = load_w(wsrc[:, 512 * t:512 * t + 512])
                for m in range(4):
                    oc = 4 * t + m
                    for (a, b) in ptiles:
                        ps = psum()
                        for kc in range(16):
                            mm(ps(a=0, b=b - a), w(kc, 128 * m, 128 * m + 128), mixb(kc, a, b), kc == 0, kc == 15)
                        tt(xT(oc, a, b), xT(oc, a, b), ps(a=0, b=b - a), ALU.add)

        proj_res(w_out)

        chk(6 + 10 * st)
        top[0] = T_MARK
        set_extra_slots(0)
        rmsnorm_fm(xT, C_GXA, hb, offp, ncm, 16, D, sq_t, rs_t)
        qT = alloc([128, 16, 640], BF16)
        ktb2 = [alloc([128, 16, NMEM], BF16), alloc([128, 16, NMEM], BF16)]
        vb2 = [alloc([128, 2, D], BF16), alloc([128, 2, D], BF16)]
        ktb, vb = ktb2[0], vb2[0]
        pT2 = [alloc([128, 2, 512], BF16), alloc([128, 2, 512], BF16)]
        rden2 = [alloc([128, 512], F32), alloc([128, 512], F32)]
        att_i = [0]
        kv_jobs = ([("s", bi) for bi in range(4)] if sample else []) + [("p", 0)]

        def kv_load(job, slot):
            kb_, vb_ = ktb2[slot], vb2[slot]
            if job[0] == "s":
                bi = job[1]
                dma("pool", kb_.ap, skT_d[bi].rearrange("(kc p) m -> p kc m", p=128), "kvlk%d" % slot, W=[kb_.all()])
                dma("pool", vb_.ap, sv_d[bi].rearrange("(mc p) n -> p mc n", p=128), "kvlv%d" % slot, W=[vb_.all()])
            else:
                dma("sp", kb_.ap.rearrange("p k n -> p (k n)"), kv_scr[:, 0:4096], "kvlk%d" % slot, R=[KV_D], W=[kb_.all()])
                dma("sp", vb_.ap.rearrange("p k n -> p (k n)"), kv_scr[:, 4096:8192], "kvlv%d" % slot, R=[KV_D], W=[vb_.all()])

        def kv_cols(job):
            if job[0] == "s":
                return [(prompt_n + 16 * job[1], prompt_n + 16 * job[1] + 16)]
            return split_cols(offp, prompt_n)

        for i in range(min(2, len(kv_jobs))):
            kv_load(kv_jobs[i], i)
        for t in range(4):
            w = load_w(w_q[:, 512 * t:512 * t + 512])
            for m in range(4):
                oc = 4 * t + m
                for (a, b) in ptiles:
                    ps = psum()
                    for kc in range(16):
                        mm(ps(a=0, b=b - a), w(kc, 128 * m, 128 * m + 128), hb(kc, a, b), kc == 0, kc == 15)
                    cp(qT(oc, a, b), ps(a=0, b=b - a), eng="act")

        def attend(cols, ktb=ktb, vb=vb):
            for hd in range(4):
                for (a, b) in cols:
                    n = b - a
                    pT, rden = pT2[att_i[0] % 2], rden2[att_i[0] % 2]
                    att_i[0] += 1
                    for mc in range(2):
                        ps = psum()
                        for dc in range(4):
                            mm(ps(a=0, b=n), ktb(4 * hd + dc, 128 * mc, 128 * mc + 128), qT(4 * hd + dc, a, b),
                               dc == 0, dc == 3)
                        act(pT(mc, 0, n), ps(a=0, b=n), AF.Exp, scale=512.0 ** -0.5)
                    ps = psum()
                    for mc in range(2):
                        mm(ps(a=0, b=n), ones_bf.all(), pT(mc, 0, n), mc == 0, mc == 1)
                    recip(rden(a=0, b=n), ps(a=0, b=n))
                    for dvc in range(4):
                        ps = psum()
                        for mc in range(2):
                            mm(ps(a=0, b=n), vb(mc, 512 * hd + 128 * dvc, 512 * hd + 128 * dvc + 128), pT(mc, 0, n),
                               mc == 0, mc == 1)
                        tt(mixb(4 * hd + dvc, a, b), ps(a=0, b=n), rden(a=0, b=n), ALU.mult)

        for i, job in enumerate(kv_jobs):
            attend(kv_cols(job), ktb2[i % 2], vb2[i % 2])
            if i + 2 < len(kv_jobs):
                kv_load(kv_jobs[i + 2], i % 2)
        proj_res(w_o)

        chk(7 + 10 * st)
        top[0] = mixb.base
        set_extra_slots(0)
        sq_t = None
        hid = alloc([128, 48, 576], BF16)
        ub2 = [alloc([128, 2, 592], F32), alloc([128, 2, 592], F32)]
        vc2 = [alloc([128, 592], F32), alloc([128, 592], F32)]
        gc2 = [alloc([128, 592], F32), alloc([128, 592], F32)]
        assert vc2[1].base == vc2[0].base + 592 * 4
        rs_t = Buf(arena[:, vc2[0].base // 4:vc2[0].base // 4 + 1184].rearrange("p (k n) -> p k n", k=2), "sb",
                   vc2[0].base, [128, 2, 592], 4)
        rmsnorm_fm(xT, C_GFFN, hb, offp, ncm, 16, D, sq_t, rs_t)
        np_prompt = prompt_n - offp
        set_extra_slots(1)
        vc4 = vc2 + gc2
        gcx = [alloc([128, 592], F32), alloc([128, 592], F32)]

        n_p = np_prompt
        n_c = n_p + (72 if sample else 0)

        def s3(buf_ap, c0):
            return buf_ap[:, c0:c0 + 72].rearrange("p (b t) -> p b t", b=4)

        def up_chunk(j, q, vg, w, ub, dst):
            ch = 48 * vg + 4 * j + q
            ubv = ub.ap[:, vg, :]
            cp(ub(vg, 0, 2), up_hist(ch), eng="act")
            if sample:
                S.op("act", lambda e, o=s3(ubv, 2 + n_p)[:, :, 0:2], i=fcst.ap[:, ch, :, :]:
                     e.activation(out=o, in_=i, func=AF.Copy), [fcst.all()], [ub(vg, 2 + n_p, 2 + n_p + 72)])
            for (a, b) in ptiles:
                ps = psum()
                for kc in range(16):
                    mm(ps(a=0, b=b - a), w(kc, 128 * q, 128 * q + 128), hb(kc, a, b), kc == 0, kc == 15)
                pa, pb = a, min(b, prompt_n)
                cp(ub(vg, 2 + pa - offp, 2 + pb - offp), ps(a=0, b=pb - pa), eng="act")
                if sample and b > prompt_n:
                    src = ps.ap[:, prompt_n - a:b - a].rearrange("p (b t) -> p b t", b=4)
                    S.op("act", lambda e, o=s3(ubv, 2 + n_p)[:, :, 2:18], i=src: e.activation(out=o, in_=i, func=AF.Copy),
                         [ps.all()], [ub(vg, 2 + n_p, 2 + n_p + 72)])
            if st == 0:
                ts(ub(vg, 2, 4), ub(vg, 2, 4), cc(C_MASK + 6), None, ALU.mult)
            cp(up_hist(ch), ub(vg, n_p, n_p + 2), eng="act")
            if st == 1:
                cp(o_ffnc.raw(o_ffnc.ap[:, ch, 0, :]), ub(vg, n_p, n_p + 2), eng="act")
            ts(dst(a=0, b=n_c), ub(vg, 0, n_c), cc(C_FCW + 3 * ch), cc(C_FCB + ch), ALU.mult, ALU.add)
            for jj in range(1, 3):
                stt(dst(a=0, b=n_c), ub(vg, jj, jj + n_c), cc(C_FCW + 3 * ch + jj), dst(a=0, b=n_c), ALU.mult, ALU.add)
            if sample:
                S.op("act", lambda e, o=o_ffnc.ap[:, ch, 1:5, :], i=s3(ubv, 2 + n_p)[:, :, 16:18]:
                     e.activation(out=o, in_=i, func=AF.Copy), [ub(vg, 2 + n_p, 2 + n_p + 72)], [o_ffnc.all()])

        for j in range(12):
            wv = load_w(w_up[:, 512 * j:512 * j + 512])
            wg = load_w(w_up[:, DFF + 512 * j:DFF + 512 * j + 512])
            for q in range(4):
                up_chunk(j, q, 0, wv, ub2[q % 2], vc4[q])
            for q in range(4):
                gc = gcx[q % 2]
                up_chunk(j, q, 1, wg, ub2[q % 2], gc)
                act(gc(a=0, b=n_c), gc(a=0, b=n_c), AF.Gelu_apprx_tanh)
                vq = vc4[q]
                tt(hid(4 * j + q, 0, n_p), vq(a=0, b=n_p), gc(a=0, b=n_p), ALU.mult)
                if sample:
                    ho = hid.ap[:, 4 * j + q, n_p:n_p + 64].rearrange("p (b t) -> p b t", b=4)
                    S.op("dve", lambda e, o=ho, i0_=s3(vq.ap, n_p + 2)[:, :, 0:16], i1_=s3(gc.ap, n_p + 2)[:, :, 0:16]:
                         e.tensor_tensor(out=o, in0=i0_, in1=i1_, op=ALU.mult),
                         [vq(a=n_p, b=n_p + 74), gc(a=n_p, b=n_p + 74)], [hid(4 * j + q, n_p, n_p + 64)])
        chk(8 + 10 * st)
        for mp in range(8):
            pss = {}
            for kh in range(2):
                w = load_w(w_down[3072 * kh:3072 * kh + 3072, 256 * mp:256 * mp + 256], kch=24, ncol=256)
                for m in range(2):
                    for ti, (a, b) in enumerate(ptiles):
                        if kh == 0:
                            pss[(m, ti)] = psum()
                        ps = pss[(m, ti)]
                        for kc in range(24):
                            mm(ps(a=0, b=b - a), w(kc, 128 * m, 128 * m + 128), hid(24 * kh + kc, a - offp, b - offp),
                               kh == 0 and kc == 0, kh == 1 and kc == 23)
            for m in range(2):
                oc = 2 * mp + m
                for ti, (a, b) in enumerate(ptiles):
                    tt(xT(oc, a, b), xT(oc, a, b), pss[(m, ti)](a=0, b=b - a), ALU.add)
        chk(9 + 10 * st)
        top[0] = mixb.base
        set_extra_slots(0)
        sq_t = None
        rs_t = alloc([128, 2, 512], F32)
        yb = alloc([128, 16, 640], F32)
        rmsnorm_fm(xT, C_GFIN, yb, offp, ncm, 16, D, sq_t, rs_t, sqb=hb)
        nout = ncm - ysrc0
        dma("sp", yT_o[:, ycol0:ycol0 + nout].rearrange("(kc p) n -> p kc n", p=128), yb.ap[:, :, ysrc0:ncm], "yout",
            R=[yb.all()])

    try:
        phases()
    except _Stop:
        pass
    dma("sp", lruh_o, o_lruh.ap.rearrange("p k b -> p (k b)"), "fin", R=[o_lruh.all()])
    dma("sp", lruc_o, o_lruc.ap.rearrange("p k b n -> p (k b n)"), "fin", R=[o_lruc.all()])
    dma("sp", ffnc_o, o_ffnc.ap.rearrange("p k b n -> p (k b n)"), "fin", R=[o_ffnc.all()])

    S.finalize()
    ENG = ["pe", "act", "dve", "pool", "sp"]
    sem_ctx = {}
    sems = {}
    for e in ENG:
        sem_ctx[e] = nc.semaphore("s_" + e)
        sems[e] = sem_ctx[e].__enter__()
    for k in S.dma_counts:
        sem_ctx[k] = nc.semaphore("d_" + k)
        sems[k] = sem_ctx[k].__enter__()

    def emit(engname, handle):
        waited = {}
        for o in S.ops:
            if o.eng != engname:
                continue
            need = {}
            for d in o.deps:
                if d.dma_sem is not None:
                    key, val = d.dma_sem, d.dma_val
                else:
                    if d.eng == engname and engname == "pe":
                        continue
                    key, val = d.eng, d.sigval
                if val > need.get(key, 0):
                    need[key] = val
            for key, val in need.items():
                if waited.get(key, 0) >= val:
                    continue
                waited[key] = val
                handle.wait_ge(sems[key], val)
            ins = o.fn(handle)
            if o.dma_sem is not None:
                ins.then_inc(sems[o.dma_sem], 16)
            elif o.signal:
                ins.then_inc(sems[engname], 1)
        if engname == "sp":
            for k, c in S.dma_counts.items():
                if waited.get(k, 0) < c:
                    handle.wait_ge(sems[k], c)

    with nc.Block() as block:
        @block.tensor
        def _(t):
            emit("pe", t)

        @block.scalar
        def _(s):
            emit("act", s)

        @block.vector
        def _(v):
            emit("dve", v)

        @block.gpsimd
        def _(g):
            emit("pool", g)

        @block.sync
        def _(sy):
            emit("sp", sy)
    return nc


_NC = None


def _fm(v, k):
    return np.ascontiguousarray(np.asarray(v, np.float32).reshape(k, 128).T)


def kernel(x_prompt, x_sample, mem_prompt, cache_mem_k, cache_mem_v, state_lru_h, state_lru_conv,
           state_ffn_conv, norm_mix, w_in, g_v, gmlp_w, gmlp_b, lru_conv_w, lru_conv_b, lru_wa, lru_ba,
           lru_wx, lru_bx, lru_lam, g_a, g_b, w_out, norm_mem, w_kv, norm_xa, w_q, w_o, norm_ffn, w_up,
           ffn_conv_w, ffn_conv_b, w_down, norm_final):
    global _NC
    f = lambda a: np.ascontiguousarray(np.asarray(a, np.float32))
    x_prompt, x_sample, mem_prompt = f(x_prompt), f(x_sample), f(mem_prompt)
    if _NC is None:
        _NC = build_program()
    nc = _NC
    shared = {
        "w_in": f(w_in[0]), "w_out": f(w_out[0]), "w_kv": f(w_kv[0]), "w_q": f(w_q[0]), "w_o": f(w_o[0]),
        "w_up": f(w_up[0]), "w_down": f(w_down[0]),
        "wmT": f(np.transpose(np.asarray(gmlp_w[0]), (2, 0, 1)).reshape(128, 1024)),
        "lwa": f(np.transpose(np.asarray(lru_wa[0]), (1, 0, 2)).reshape(128, 1024)),
        "lwx": f(np.transpose(np.asarray(lru_wx[0]), (1, 0, 2)).reshape(128, 1024)),
    }
    cstb = np.zeros((128, 2560), np.float32)
    cstb[:, 0:1024] = np.asarray(g_v[0])[None, :]
    cstb[:, 1024:2048] = np.asarray(gmlp_b[0]).reshape(1, 1024)
    cstb[:, 2048:2560] = np.tile(np.asarray(gmlp_b[0])[:, None, 0:16], (1, 4, 1)).reshape(1, 512)
    shared["cstb"] = cstb
    cbase = np.zeros((128, NCST), np.float32)
    cbase[:, C_GMIX:C_GMIX + 16] = _fm(norm_mix[0], 16)
    cbase[:, C_GXA:C_GXA + 16] = _fm(norm_xa[0], 16)
    cbase[:, C_GFFN:C_GFFN + 16] = _fm(norm_ffn[0], 16)
    cbase[:, C_GFIN:C_GFIN + 16] = _fm(norm_final, 16)
    cbase[:, C_GMEM:C_GMEM + 16] = _fm(norm_mem[0], 16)
    cbase[:, C_GA:C_GA + 8] = _fm(g_a[0], 8)
    cbase[:, C_GB:C_GB + 8] = _fm(g_b[0], 8)
    cw = np.asarray(lru_conv_w[0], np.float32)
    cbase[:, C_CW:C_CW + 32] = cw.reshape(4, 8, 128).transpose(2, 1, 0).reshape(128, 32)
    cbase[:, C_CB:C_CB + 8] = _fm(lru_conv_b[0], 8)
    cbase[:, C_BA:C_BA + 8] = _fm(lru_ba[0], 8)
    cbase[:, C_BX:C_BX + 8] = _fm(lru_bx[0], 8)
    cbase[:, C_LAM:C_LAM + 8] = _fm(lru_lam[0], 8)
    fw = np.asarray(ffn_conv_w[0], np.float32)
    cbase[:, C_FCW:C_FCW + 288] = fw.reshape(3, 96, 128).transpose(2, 1, 0).reshape(128, 288)
    cbase[:, C_FCB:C_FCB + 96] = _fm(ffn_conv_b[0], 96)
    in_maps = []
    for c in range(8):
        b, s = c // 4, c % 4
        start = 1024 * s
        xin = np.zeros((NIN, D), np.float32)
        lo = start - (NPRE + NHALO)
        src_lo = max(lo, 0)
        xin[src_lo - lo:NPRE + NHALO + NP_] = x_prompt[b, src_lo:start + NP_]
        xin[NPRE + NHALO + NP_:] = x_sample[4 * c:4 * c + 4].reshape(NS, D)
        cst = cbase.copy()
        for ti, (c0, c1) in enumerate(PRE_TILES):
            cst[:, C_MASK + ti] = 1.0 if (lo + c0) >= 0 else 0.0
        cst[:, C_MASK + 6] = 1.0 if s >= 1 else 0.0
        hs = np.asarray(state_lru_h[0, 4 * c:4 * c + 4], np.float32)
        cst[:, C_HST:C_HST + 32] = hs.reshape(4, 8, 128).transpose(2, 1, 0).reshape(128, 32)
        sxr = np.asarray(state_lru_conv[0, 4 * c:4 * c + 4], np.float32)
        sfc = np.asarray(state_ffn_conv[0, 4 * c:4 * c + 4], np.float32)
        m = {
            "xin": np.ascontiguousarray(xin.T),
            "memT": np.ascontiguousarray(mem_prompt[b].T),
            "cst": cst,
            "skT": np.ascontiguousarray(np.asarray(cache_mem_k[0, 4 * c:4 * c + 4], np.float32).reshape(4, NMEM, D).transpose(0, 2, 1)),
            "sv": np.ascontiguousarray(np.asarray(cache_mem_v[0, 4 * c:4 * c + 4], np.float32).reshape(4, NMEM, D)),
            "sxr": np.ascontiguousarray(sxr.reshape(4, 3, 8, 128).transpose(3, 2, 0, 1).reshape(128, 96)),
            "sfc": np.ascontiguousarray(sfc.reshape(4, 2, 96, 128).transpose(3, 2, 0, 1).reshape(128, 768)),
        }
        m.update(shared)
        in_maps.append(m)
    res = run_bass_kernel_spmd(nc, in_maps, core_ids=list(range(8)))
    R = res.results
    y_prompt = np.zeros((2, 4096, D), np.float32)
    y_sample = np.zeros((32, 16, D), np.float32)
    mk = np.zeros((1, 2, NMEM, 4, 512), np.float32)
    mv = np.zeros((1, 2, NMEM, 4, 512), np.float32)
    lh_p = np.zeros((1, 2, DB), np.float32)
    lc_p = np.zeros((1, 2, 3, DB), np.float32)
    fc_p = np.zeros((1, 2, 2, 2 * DFF), np.float32)
    lh_s = np.zeros((1, 32, DB), np.float32)
    lc_s = np.zeros((1, 32, 3, DB), np.float32)
    fc_s = np.zeros((1, 32, 2, 2 * DFF), np.float32)
    gv_s = np.zeros((1, 32, 16, DA), np.float32)
    for c in range(8):
        b, s = c // 4, c % 4
        r = R[c]
        yT = np.asarray(r["yT"])
        y_prompt[b, 1024 * s:1024 * s + 1024] = yT[:, 0:1024].T
        y_sample[4 * c:4 * c + 4] = yT[:, 1024:1088].T.reshape(4, 16, D)
        lruh = np.asarray(r["lruh"]).reshape(128, 8, 5)
        lruc = np.asarray(r["lruc"]).reshape(128, 8, 5, 3)
        ffnc = np.asarray(r["ffnc"]).reshape(128, 96, 5, 2)
        if s == 0:
            mk[0, b] = np.asarray(r["mkT"]).T.reshape(NMEM, 4, 512)
            mv[0, b] = np.asarray(r["mv"]).reshape(NMEM, 4, 512)
        if s == 3:
            lh_p[0, b] = lruh[:, :, 0].T.reshape(DB)
            lc_p[0, b] = lruc[:, :, 0, :].transpose(2, 1, 0).reshape(3, DB)
            fc_p[0, b] = ffnc[:, :, 0, :].transpose(2, 1, 0).reshape(2, 2 * DFF)
        lh_s[0, 4 * c:4 * c + 4] = lruh[:, :, 1:5].transpose(2, 1, 0).reshape(4, DB)
        lc_s[0, 4 * c:4 * c + 4] = lruc[:, :, 1:5, :].transpose(2, 3, 1, 0).reshape(4, 3, DB)
        fc_s[0, 4 * c:4 * c + 4] = ffnc[:, :, 1:5, :].transpose(2, 3, 1, 0).reshape(4, 2, 2 * DFF)
        gv_s[0, 4 * c:4 * c + 4] = np.asarray(r["gv"]).reshape(4, 16, DA)
    return (y_prompt, y_sample, mk, mv, lh_p, lc_p, fc_p, lh_s, lc_s, fc_s, gv_s)
```
